# Optimizing a Trainium2 kernel written in Bass

```python
import math
import jax, jax.numpy as jnp
from jax import lax
import numpy as np

D_MODEL = 2048
BATCH = 2
SEQ = 4096
DEPTH = 1

N_HEADS = 16
HEAD_DIM = 128
N_KV_HEADS = 4
KV_REP = N_HEADS // N_KV_HEADS
D_ATTN = N_HEADS * HEAD_DIM
D_KV = N_KV_HEADS * HEAD_DIM
IDX_HEADS = 8
IDX_DIM = 64
TOPK_MAX = 256
Q_BLOCK = 128
SGU_CHUNK = 128
SGU_GROUPS = 16
D_SGU = D_MODEL
SGU_GROUP_DIM = D_SGU // SGU_GROUPS
D_FF = 4 * D_MODEL
REL_BUCKETS = 32
REL_MAX_DIST = 128
EPS = 1e-6
N_BRANCHES = 2
COL_SIZES = (D_ATTN, D_KV, D_KV, IDX_HEADS * IDX_DIM, IDX_DIM, IDX_HEADS,
             D_SGU, D_SGU, D_MODEL, D_MODEL)
D_IN = 2 * D_ATTN // 2 + 2 * D_KV + IDX_HEADS * IDX_DIM + IDX_DIM + IDX_HEADS + 2 * D_SGU + N_BRANCHES * D_MODEL

kernel_name = "hybrid_dsa_sgu_gated_block"


def rms_norm(x, g):
    xf = x.astype(jnp.float32)
    y = xf * lax.rsqrt(jnp.mean(xf * xf, axis=-1, keepdims=True) + EPS)
    return (y * g.astype(jnp.float32)).astype(x.dtype)


def layer_norm(x, g, b):
    xf = x.astype(jnp.float32)
    mu = jnp.mean(xf, axis=-1, keepdims=True)
    var = jnp.mean(jnp.square(xf - mu), axis=-1, keepdims=True)
    y = (xf - mu) * lax.rsqrt(var + EPS)
    return (y * g.astype(jnp.float32) + b.astype(jnp.float32)).astype(x.dtype)


def rel_bucket(dist):
    n = jnp.maximum(dist, 0)
    max_exact = REL_BUCKETS // 2
    nf = jnp.maximum(n, 1).astype(jnp.float32)
    large = max_exact + (jnp.log(nf / max_exact) / math.log(REL_MAX_DIST / max_exact)
                         * (REL_BUCKETS - max_exact)).astype(jnp.int32)
    large = jnp.minimum(large, REL_BUCKETS - 1)
    return jnp.where(n < max_exact, n, large)


def sparse_attention(q, k, v, qi, ki, wi, rel_bias):
    B, T = q.shape[0], q.shape[1]
    L = k.shape[1]
    top_k = min(TOPK_MAX, L // 4)
    nb = T // Q_BLOCK

    def to_blocks(a):
        return a.reshape((B, nb, Q_BLOCK) + a.shape[2:]).swapaxes(0, 1)

    starts = jnp.arange(nb, dtype=jnp.int32) * Q_BLOCK
    key_pos = jnp.arange(L, dtype=jnp.int32)
    ki32 = ki.astype(jnp.float32)
    idx_scale = (IDX_HEADS ** -0.5) * (IDX_DIM ** -0.5)

    def block(args):
        qb, qib, wib, start = args
        q_pos = start + jnp.arange(Q_BLOCK, dtype=jnp.int32)
        s = jnp.einsum('bqhd,bsd->bqhs', qib.astype(jnp.float32), ki32)
        score = jnp.einsum('bqh,bqhs->bqs', wib.astype(jnp.float32) * idx_scale, jax.nn.relu(s))
        causal = key_pos[None, :] <= q_pos[:, None]
        score = jnp.where(causal[None], score, -jnp.inf)
        _, idx = lax.top_k(score, top_k)
        valid = idx <= q_pos[None, :, None]
        k_sel = jax.vmap(lambda kb, ib: kb[ib])(k, idx)
        v_sel = jax.vmap(lambda vb, ib: vb[ib])(v, idx)
        qg = qb.reshape(B, Q_BLOCK, N_KV_HEADS, KV_REP, HEAD_DIM)
        logits = jnp.einsum('bqgrd,bqngd->bqgrn', qg, k_sel).astype(jnp.float32) * (HEAD_DIM ** -0.5)
        bucket = rel_bucket(q_pos[None, :, None] - idx)
        bias = rel_bias[bucket].astype(jnp.float32)
        bias = bias.reshape(B, Q_BLOCK, top_k, N_KV_HEADS, KV_REP).transpose(0, 1, 3, 4, 2)
        logits = jnp.where(valid[:, :, None, None, :], logits + bias, -1e30)
        p = jax.nn.softmax(logits, axis=-1).astype(v.dtype)
        o = jnp.einsum('bqgrn,bqngd->bqgrd', p, v_sel)
        return o.reshape(B, Q_BLOCK, D_ATTN)

    out = lax.map(block, (to_blocks(q), to_blocks(qi), to_blocks(wi), starts))
    return out.swapaxes(0, 1).reshape(B, T, D_ATTN)


def spatial_gating(u, v, ln_g, ln_b, w_s, b_s):
    B, T = v.shape[0], v.shape[1]
    nc = T // SGU_CHUNK
    vn = layer_norm(v, ln_g, ln_b)
    vc = vn.reshape(B, nc, SGU_CHUNK, SGU_GROUPS, SGU_GROUP_DIM)
    mask = jnp.tril(jnp.ones((SGU_CHUNK, SGU_CHUNK), dtype=bool))
    ws = jnp.where(mask[None], w_s, jnp.zeros_like(w_s))
    mixed = jnp.einsum('gts,bcsgd->bctgd', ws, vc) + b_s.T[None, None, :, :, None]
    return u * mixed.reshape(B, T, D_SGU)


def setup_inputs(seed: int = 0) -> dict:
    key = jax.random.key(seed)
    ks = jax.random.split(key, 16)
    f32 = jnp.float32
    d_in = sum(COL_SIZES)
    x = jax.random.normal(ks[0], (BATCH, SEQ, D_MODEL), f32)
    rel_bias = 0.5 * jax.random.normal(ks[1], (REL_BUCKETS, N_HEADS), f32)
    norm1_g = 1.0 + 0.01 * jax.random.normal(ks[2], (DEPTH, D_MODEL), f32)
    w_in = jax.random.normal(ks[3], (DEPTH, D_MODEL, d_in), f32) * D_MODEL ** -0.5
    sgu_ln_g = 1.0 + 0.01 * jax.random.normal(ks[4], (DEPTH, D_SGU), f32)
    sgu_ln_b = 0.01 * jax.random.normal(ks[5], (DEPTH, D_SGU), f32)
    sgu_w = jax.random.normal(ks[6], (DEPTH, SGU_GROUPS, SGU_CHUNK, SGU_CHUNK), f32) * SGU_CHUNK ** -0.5
    sgu_b = 1.0 + 0.01 * jax.random.normal(ks[7], (DEPTH, SGU_GROUPS, SGU_CHUNK), f32)
    w_out = jax.random.normal(ks[8], (DEPTH, D_MODEL, D_MODEL), f32) * D_MODEL ** -0.5
    norm2_g = 1.0 + 0.01 * jax.random.normal(ks[9], (DEPTH, D_MODEL), f32)
    w_ff1 = jax.random.normal(ks[10], (DEPTH, D_MODEL, D_FF), f32) * D_MODEL ** -0.5
    w_ff2 = jax.random.normal(ks[11], (DEPTH, D_FF, D_MODEL), f32) * D_FF ** -0.5
    final_g = 1.0 + 0.01 * jax.random.normal(ks[12], (D_MODEL,), f32)
    return {"x": x, "rel_bias": rel_bias, "norm1_g": norm1_g, "w_in": w_in,
            "sgu_ln_g": sgu_ln_g, "sgu_ln_b": sgu_ln_b, "sgu_w": sgu_w, "sgu_b": sgu_b,
            "w_out": w_out, "norm2_g": norm2_g, "w_ff1": w_ff1, "w_ff2": w_ff2,
            "final_g": final_g}


def reference(x, rel_bias, norm1_g, w_in, sgu_ln_g, sgu_ln_b, sgu_w, sgu_b,
              w_out, norm2_g, w_ff1, w_ff2, final_g):
    B, T, _ = x.shape
    split_pts = [int(p) for p in np.cumsum(COL_SIZES)[:-1]]
    for l in range(DEPTH):
        h = rms_norm(x, norm1_g[l])
        proj = jnp.einsum('btd,dc->btc', h, w_in[l])
        q, k, v, qi, ki, wi, u_s, v_s, ga, gb = jnp.split(proj, split_pts, axis=-1)
        q = q.reshape(B, T, N_HEADS, HEAD_DIM)
        k = k.reshape(B, T, N_KV_HEADS, HEAD_DIM)
        v = v.reshape(B, T, N_KV_HEADS, HEAD_DIM)
        qi = qi.reshape(B, T, IDX_HEADS, IDX_DIM)
        attn = sparse_attention(q, k, v, qi, ki, wi, rel_bias)
        sgu = spatial_gating(jax.nn.gelu(u_s), jax.nn.gelu(v_s),
                             sgu_ln_g[l], sgu_ln_b[l], sgu_w[l], sgu_b[l])
        merged = jax.nn.sigmoid(ga) * attn + jax.nn.sigmoid(gb) * sgu
        x = x + jnp.einsum('btc,cd->btd', merged, w_out[l])
        h2 = rms_norm(x, norm2_g[l])
        ff = jnp.square(jax.nn.relu(jnp.einsum('btd,df->btf', h2, w_ff1[l])))
        x = x + jnp.einsum('btf,fd->btd', ff, w_ff2[l])
    return rms_norm(x, final_g)
```

```python
import os
from contextlib import ExitStack

import numpy as np
import ml_dtypes

import concourse.bass as bass
import concourse.mybir as mybir
from concourse.bass_utils import run_bass_kernel_spmd

F32, BF16 = mybir.dt.float32, mybir.dt.bfloat16
ALU, AF, AX = mybir.AluOpType, mybir.ActivationFunctionType, mybir.AxisListType

D = 2048
SEQ = 4096
NBLK = 32
D_IN = 11848
C_Q, C_K, C_V, C_QI, C_KI, C_WI, C_U, C_VS, C_GA, C_GB = (
    0, 2048, 2560, 3072, 3584, 3648, 3656, 5704, 7752, 9800)
DFF = 8192
EPS = 1e-6
TOPK = 256
NBIS = 20
SCALE = 128 ** -0.5
GC = 0.7978845608028654
GA_ = 0.044715
NEG = -1.0e30
BIGM = 30000.0

COMPUTE = ("pe", "act", "dve", "pool")


class Buf:
    __slots__ = ("w", "r", "name")

    def __init__(self, name=""):
        self.w = []
        self.r = {}
        self.name = name


class Prog:
    def __init__(self, nc, es, ndsem=16):
        self.nc = nc
        self.streams = {e: [] for e in ("pe", "act", "dve", "pool", "sp")}
        self.cnt = {e: 0 for e in COMPUTE}
        self.waited = {e: {} for e in self.streams}
        self.sem = {}
        for e in COMPUTE:
            self.sem[e] = es.enter_context(nc.semaphore("s_" + e))
        self.K = ndsem
        self.dn = {"sp": 0, "pool": 0}
        for q in ("sp", "pool"):
            for k in range(ndsem):
                self.sem[(q, k)] = es.enter_context(nc.semaphore("d_%s%d" % (q, k)))

    def _deps(self, eng, reads, writes, shared=False):
        deps = {}
        for b in reads:
            for (k, v) in b.w:
                if deps.get(k, 0) < v:
                    deps[k] = v
        for b in writes:
            if not shared:
                for (k, v) in b.w:
                    if deps.get(k, 0) < v:
                        deps[k] = v
            for k, v in b.r.items():
                if deps.get(k, 0) < v:
                    deps[k] = v
        waits = []
        wd = self.waited[eng]
        for k, v in deps.items():
            if eng == "pe" and k == "pe":
                continue
            if wd.get(k, 0) >= v:
                continue
            wd[k] = v
            waits.append((k, v))
        return waits

    def op(self, eng, fn, reads=(), writes=(), inc=True):
        waits = self._deps(eng, reads, writes)
        if inc:
            self.cnt[eng] += 1
            tv = self.cnt[eng]
        else:
            tv = self.cnt[eng] + 1
        self.streams[eng].append((waits, fn, (eng, 1) if inc else None))
        for b in reads:
            if b.r.get(eng, 0) < tv:
                b.r[eng] = tv
        for b in writes:
            b.w = [(eng, tv)]
            b.r = {}

    def dma(self, q, out, in_, reads=(), writes=(), shared=False):
        waits = self._deps(q, reads, writes, shared)
        n = self.dn[q]
        self.dn[q] = n + 1
        k = n % self.K
        val = 16 * (n // self.K + 1)
        key = (q, k)
        if n >= self.K and self.waited[q].get(key, 0) < val - 16:
            self.waited[q][key] = val - 16
            waits.append((key, val - 16))
        self.streams[q].append(
            (waits, lambda e, o=out, i=in_: e.dma_start(out=o, in_=i), (key, 16)))
        for b in reads:
            if b.r.get(key, 0) < val:
                b.r[key] = val
        for b in writes:
            if shared:
                b.w = list(b.w) + [(key, val)]
            else:
                b.w = [(key, val)]
            b.r = {}

    def barrier(self):
        waits = []
        for q in ("sp", "pool"):
            n = self.dn[q]
            for k in range(min(n, self.K)):
                last = ((n - 1 - k) // self.K) * self.K + k
                waits.append(((q, k), 16 * (last // self.K + 1)))
        for e in COMPUTE:
            if self.cnt[e]:
                waits.append((e, self.cnt[e]))
        for e in self.streams:
            ws = []
            for k, v in waits:
                if e == "pe" and k == "pe":
                    continue
                if self.waited[e].get(k, 0) >= v:
                    continue
                self.waited[e][k] = v
                ws.append((k, v))
            if ws:
                self.streams[e].append((ws, None, None))

    def finish(self):
        waits = []
        for q in ("sp", "pool"):
            n = self.dn[q]
            for k in range(min(n, self.K)):
                last = ((n - 1 - k) // self.K) * self.K + k
                waits.append(((q, k), 16 * (last // self.K + 1)))
        for e in COMPUTE:
            if self.cnt[e]:
                waits.append((e, self.cnt[e]))
        self.streams["sp"].append((waits, None, None))

    def emit(self):
        nc = self.nc
        with nc.Block() as block:
            def mk(name):
                def body(eng):
                    for waits, fn, inc in self.streams[name]:
                        for k, v in waits:
                            eng.wait_ge(self.sem[k], v)
                        if fn is None:
                            continue
                        ins = fn(eng)
                        if inc is not None:
                            ins.then_inc(self.sem[inc[0]], inc[1])
                return body
            block.tensor(mk("pe"))
            block.scalar(mk("act"))
            block.vector(mk("dve"))
            block.gpsimd(mk("pool"))
            block.sync(mk("sp"))


def build_program(stage=99, debug=False, nsb=2):
    nc = bass.Bass("TRN2", target_bir_lowering=False)
    es = ExitStack()
    P = Prog(nc, es)

    def din(name, shape, dt=F32):
        return nc.dram_tensor(name, list(shape), dt, kind="ExternalInput").ap()

    def dout(name, shape, dt=F32):
        return nc.dram_tensor(name, list(shape), dt, kind="ExternalOutput").ap()

    def dscr(name, shape, dt):
        kind = "ExternalOutput" if debug else "Internal"
        return nc.dram_tensor(name, list(shape), dt, kind=kind).ap()

    def sb(name, shape, dt, stack=None):
        return (stack or es).enter_context(nc.sbuf_tensor("s_" + name, list(shape), dt))

    x_sh = din("x_sh", [SEQ, D])
    w_in = din("w_in", [D, D_IN])
    w_out = din("w_out", [D, D])
    w_ff1 = din("w_ff1", [D, DFF])
    w_ff2 = din("w_ff2", [DFF, D])
    g1_d = din("norm1_g", [1, D])
    g2_d = din("norm2_g", [1, D])
    fg_d = din("final_g", [1, D])
    lng_d = din("lng_col", [128, 16])
    lnb_d = din("lnb_col", [128, 16])
    sgub_d = din("sgub_row", [1, D])
    wsT_d = din("wsT", [128, 16 * 128])
    tril_d = din("trilT", [128, 128])
    ident_d = din("ident_h", [128, 128], BF16)
    negm_d = din("negmask", [128, 128])
    fake_d = din("fakemask", [128, 512])
    braw_d = din("braw", [128, 2 * 16 * 128])
    cfar_d = din("cfar", [128, 16])
    pow2_d = din("pow2", [128, NBIS + 1])
    y_out = dout("y", [1024, D])

    kT_d = dscr("kT_scr", [4, 128, SEQ], BF16)
    V_d = dscr("V_scr", [SEQ, 512], BF16)
    dbg = {}

    def dump(name, ap, shape, dt, reads):
        if not debug:
            return
        d = dout("dbg_" + name, shape, dt)
        P.dma("sp", d, ap, reads=reads)

    psf = []
    for i in range(6):
        t = es.enter_context(nc.psum_tensor("psf%d" % i, [128, 512], F32))
        psf.append((t, Buf("psf%d" % i)))
    psb = []
    for i in range(2):
        t = es.enter_context(nc.psum_tensor("psb%d" % i, [128, 8, 128], BF16))
        psb.append((t, Buf("psb%d" % i)))
    rot = {"f": 0, "b": 0, "s": 0, "a": 0}

    def next_psf():
        i = rot["f"] % 6
        rot["f"] += 1
        return psf[i]

    def next_ps4():
        i = rot["s"] % 4
        rot["s"] += 1
        return psf[i]

    def next_psb():
        i = rot["b"] % 2
        rot["b"] += 1
        return psb[i]

    def mm(out, lhsT, rhs, start, stop, reads, writes, inc=None):
        if inc is None:
            inc = stop
        P.op("pe", lambda e, o=out, l=lhsT, r=rhs, s=start, t=stop:
             e.matmul(o, l, r, start=s, stop=t), reads, writes, inc)

    def act(out, in_, func, reads, writes, bias=None, scale=None, accum=None):
        kw = {}
        if bias is not None:
            kw["bias"] = bias
        if scale is not None:
            kw["scale"] = scale
        if accum is not None:
            kw["accum_out"] = accum
        P.op("act", lambda e, o=out, i=in_, f=func, kw=kw: e.activation(o, i, f, **kw),
             reads, writes)

    def ts(eng, out, in0, s1, s2, op0, op1, reads, writes, accum=None):
        kw = {}
        if op1 is not None:
            kw["op1"] = op1
        if accum is not None:
            kw["accum_out"] = accum
        P.op(eng, lambda e, o=out, i=in0, a=s1, b=s2, p=op0, kw=kw:
             e.tensor_scalar(o, i, a, b, p, **kw), reads, writes)

    def stt(eng, out, in0, scalar, in1, op0, op1, reads, writes):
        P.op(eng, lambda e, o=out, i=in0, s=scalar, j=in1, p=op0, q=op1:
             e.scalar_tensor_tensor(o, i, s, j, p, q), reads, writes)

    def tt(eng, out, in0, in1, op, reads, writes):
        P.op(eng, lambda e, o=out, i=in0, j=in1, p=op: e.tensor_tensor(o, i, j, p),
             reads, writes)

    def copy(eng, out, in_, reads, writes):
        if eng == "act":
            P.op("act", lambda e, o=out, i=in_: e.copy(o, i), reads, writes)
        else:
            P.op(eng, lambda e, o=out, i=in_: e.tensor_copy(o, i), reads, writes)

    def memset(eng, ap, val, writes):
        P.op(eng, lambda e, o=ap, v=val: e.memset(o, v), (), writes)

    def reduce(out, in_, op, reads, writes):
        P.op("dve", lambda e, o=out, i=in_, p=op: e.tensor_reduce(o, i, AX.X, p), reads, writes)

    def wtile_dma(dst, src, writes):
        P.dma("pool", dst, src.rearrange("(k p) c -> p k c", p=128), writes=writes)

    ident = sb("ident", [128, 128], BF16)
    b_ident = Buf("ident")
    P.dma("sp", ident[:], ident_d, writes=[b_ident])
    kiT2 = sb("kiT2", [128, SEQ], BF16)
    b_kiT2 = [Buf("kiT2_%d" % g) for g in range(8)]
    ss = sb("ss", [128, 8], F32)
    b_ss = Buf("ss")
    nhalf = sb("nhalf", [128, 1], F32)
    b_nhalf = Buf("nhalf")
    P.op("pool", lambda e: e.memset(nhalf[:], -0.5), (), [b_nhalf])
    junk = sb("junk", [128, 2048], BF16)
    b_junk = Buf("junk")
    junkb = sb("junkb", [128, 2048], BF16)
    b_junkb = Buf("junkb")
    b_ssc = [Buf("ss%d" % i) for i in range(8)]
    negm = sb("negm", [128, 128], F32)
    fake = sb("fake", [128, 512], F32)
    pow2 = sb("pow2", [128, NBIS + 1], F32)
    lngc = sb("lngc", [128, 16], F32)
    lnbc = sb("lnbc", [128, 16], F32)
    b_cst = Buf("consts")
    P.dma("sp", negm[:], negm_d, writes=[b_cst], shared=True)
    P.dma("sp", fake[:], fake_d, writes=[b_cst], shared=True)
    P.dma("sp", pow2[:], pow2_d, writes=[b_cst], shared=True)
    P.dma("sp", lngc[:], lng_d, writes=[b_cst], shared=True)
    P.dma("sp", lnbc[:], lnb_d, writes=[b_cst], shared=True)
    ones_bf = sb("ones_bf", [128, 128], BF16)
    b_ones = Buf("ones")
    memset("dve", ones_bf[:], 1.0, [b_ones])
    biasT = sb("biasT", [128, 2, 16, 128], BF16)
    b_biasT = Buf("biasT")
    wsT = sb("wsT", [128, 16 * 128], BF16)
    b_wsT = Buf("wsT")
    bmix = sb("bmix", [128, 16 * 128], F32)
    b_bmix = Buf("bmix")

    if stage >= 2:
        with ExitStack() as s0:
            braw = sb("braw", [128, 2 * 16 * 128], F32, s0)
            cfar = sb("cfar", [128, 16], F32, s0)
            wsf = sb("wsf", [128, 16 * 128], F32, s0)
            tril = sb("tril", [128, 128], F32, s0)
            b_tmp = Buf("setup_tmp")
            P.dma("sp", braw[:], braw_d, writes=[b_tmp], shared=True)
            P.dma("sp", cfar[:], cfar_d, writes=[b_tmp], shared=True)
            P.dma("sp", wsf[:], wsT_d, writes=[b_tmp], shared=True)
            P.dma("sp", tril[:], tril_d, writes=[b_tmp], shared=True)
            P.dma("sp", bmix[:], sgub_d.partition_broadcast(128), writes=[b_bmix])
            b_tmp2 = Buf("setup_tmp2")
            for k in range(2):
                v = braw[:, k * 2048:(k + 1) * 2048].rearrange("p (h t) -> p h t", h=16)
                tt("dve", v, v, cfar[:, :].unsqueeze(2).to_broadcast([128, 16, 128]),
                   ALU.subtract, [b_tmp], [b_tmp2])
                ts("dve", biasT[:, k, :, :], v, 1.0 / SCALE, None, ALU.mult, None,
                   [b_tmp2], [b_biasT])
            tt("dve", wsT[:].rearrange("p (g t) -> p g t", g=16),
               wsf[:].rearrange("p (g t) -> p g t", g=16),
               tril[:, :].unsqueeze(1).to_broadcast([128, 16, 128]), ALU.mult,
               [b_tmp], [b_wsT])
            for q in range(4):
                pt, bp = next_psf()
                mm(pt[:], ones_bf[:], wsT[:, q * 512:(q + 1) * 512], True, True,
                   [b_ones, b_wsT], [bp])
                for gg in range(4):
                    g = q * 4 + gg
                    stt("dve", bmix[:, g * 128:(g + 1) * 128], pt[:, gg * 128:(gg + 1) * 128],
                        lnbc[:, g:g + 1], bmix[:, g * 128:(g + 1) * 128], ALU.mult, ALU.add,
                        [bp, b_cst, b_bmix], [b_bmix])
        P.barrier()
        dump("biasT", biasT[:], [128, 2, 16, 128], BF16, [b_biasT])
        dump("bmix", bmix[:], [128, 2048], F32, [b_bmix])

    def norm_transpose(xt, b_xt, gbc, b_gbc, hb, b_hb, hT_dst, b_hT_list, st_col):
        col = ss[:, st_col:st_col + 1]
        bsc = b_ssc[st_col]
        jk, bjk = (junk, b_junk) if st_col % 2 == 0 else (junkb, b_junkb)
        act(jk[:], xt, AF.Square, [b_xt], [bjk, bsc], accum=col)
        ts("dve", col, col, 1.0 / D, EPS, ALU.mult, ALU.add, [bsc], [bsc])
        tt("pool", col, col, nhalf[:, 0:1], ALU.pow, [bsc, b_nhalf], [bsc])
        stt("dve", hb, xt, col, gbc, ALU.mult, ALU.mult, [b_xt, bsc, b_gbc], [b_hb])
        for half in range(2):
            pt, bp = next_psb()
            for kk in range(8):
                k = half * 8 + kk
                P.op("pe", lambda e, o=pt[:, kk, :], i=hb[:, k * 128:(k + 1) * 128]:
                     e.transpose(o, i, ident[:]), [b_hb, b_ident], [bp], inc=(kk == 7))
            copy("act" if half == 0 else "dve", hT_dst[:, half * 8:(half + 1) * 8, :],
                 pt[:], [bp], b_hT_list[half * 8:(half + 1) * 8])

    b_kscr = [Buf("kscr%d" % g) for g in range(8)]
    b_vscr = [Buf("vscr%d" % g) for g in range(8)]

    with ExitStack() as s1:
        g1bc = sb("g1bc", [128, D], F32, s1)
        b_g1bc = Buf("g1bc")
        P.dma("sp", g1bc[:], g1_d.partition_broadcast(128), writes=[b_g1bc])
        wk = sb("wk", [128, 16, 512], BF16, s1)
        wv = sb("wv", [128, 16, 512], BF16, s1)
        wki = sb("wki", [128, 16, 128], BF16, s1)
        b_wk, b_wv, b_wki = Buf("wk"), Buf("wv"), Buf("wki")
        b_wkg = [Buf("wk%d" % g) for g in range(4)]
        for g in range(4):
            P.dma("pool", wk[:, :, g * 128:(g + 1) * 128],
                  w_in[:, C_K + g * 128:C_K + (g + 1) * 128].rearrange("(k p) c -> p k c", p=128),
                  writes=[b_wkg[g]])
        wtile_dma(wv[:], w_in[:, C_V:C_V + 512], [b_wv])
        P.dma("pool", wki[:, :, 0:64],
              w_in[:, C_KI:C_KI + 64].rearrange("(k p) c -> p k c", p=128),
              writes=[b_wki], shared=True)
        P.dma("pool", wki[:, :, 64:128],
              w_in[:, C_KI:C_KI + 64].rearrange("(k p) c -> p k c", p=128),
              writes=[b_wki], shared=True)
        xb = [sb("xb%d" % i, [128, D], F32, s1) for i in range(4)]
        b_xb = [Buf("xb%d" % i) for i in range(4)]
        hb = [sb("hb%d" % i, [128, D], BF16, s1) for i in range(4)]
        b_hb = [Buf("hb%d" % i) for i in range(4)]
        hT = [sb("hT%d" % i, [128, 16, 512], BF16, s1) for i in range(2)]
        b_hT = [[Buf("hT%d_%d" % (i, b)) for b in range(4)] for i in range(2)]
        kst = [sb("kst%d" % i, [128, 512], BF16, s1) for i in range(2)]
        b_kst = [Buf("kst%d" % i) for i in range(2)]
        nst = [0]
        ngrp = 8 if stage >= 1 else 0

        def p1_stats(grp):
            for b in range(4):
                blk = grp * 4 + b
                P.dma("sp", xb[b][:], x_sh[blk * 128:(blk + 1) * 128, :], writes=[b_xb[b]])
                col = ss[:, (blk % 8):(blk % 8) + 1]
                bsc = b_ssc[blk % 8]
                jk, bjk = (junk, b_junk) if b % 2 == 0 else (junkb, b_junkb)
                act(jk[:], xb[b][:], AF.Square, [b_xb[b]], [bjk, bsc], accum=col)
                ts("dve", col, col, 1.0 / D, EPS, ALU.mult, ALU.add, [bsc], [bsc])
                tt("pool", col, col, nhalf[:, 0:1], ALU.pow, [bsc, b_nhalf], [bsc])
                stt("dve", hb[b][:], xb[b][:], col, g1bc[:], ALU.mult, ALU.mult,
                    [b_xb[b], bsc, b_g1bc], [b_hb[b]])

        def p1_transposes(grp, blocks):
            hTt, bhT = hT[grp % 2], b_hT[grp % 2]
            for b in blocks:
                for half in range(2):
                    pt, bp = next_psb()
                    for kk in range(8):
                        k = half * 8 + kk
                        P.op("pe", lambda e, o=pt[:, kk, :], i=hb[b][:, k * 128:(k + 1) * 128]:
                             e.transpose(o, i, ident[:]), [b_hb[b], b_ident], [bp], inc=(kk == 7))
                    copy("act" if half == 0 else "dve",
                         hTt[:, half * 8:(half + 1) * 8, b * 128:(b + 1) * 128], pt[:], [bp], [bhT[b]])

        def p1_kT(grp):
            hTt, bhT = hT[grp % 2], b_hT[grp % 2]
            for g in range(4):
                pt, bp = next_psf()
                for k in range(16):
                    mm(pt[:], wk[:, k, g * 128:(g + 1) * 128], hTt[:, k, :], k == 0, k == 15,
                       [b_wkg[g]] + bhT, [bp])
                si = nst[0] % 2
                nst[0] += 1
                copy("act" if g % 2 == 0 else "dve", kst[si][:], pt[:], [bp], [b_kst[si]])
                P.dma("sp", kT_d[g, :, grp * 512:(grp + 1) * 512], kst[si][:],
                      reads=[b_kst[si]], writes=[b_kscr[grp]], shared=True)

        def p1_ki(grp):
            hTt, bhT = hT[grp % 2], b_hT[grp % 2]
            pt, bp = next_psf()
            for k in range(16):
                mm(pt[:], wki[:, k, :], hTt[:, k, :], k == 0, k == 15, [b_wki] + bhT, [bp])
            copy("act", kiT2[:, grp * 512:(grp + 1) * 512], pt[:], [bp], [b_kiT2[grp]])

        def p1_V(grp, blocks):
            hTt, bhT = hT[grp % 2], b_hT[grp % 2]
            for b in blocks:
                blk = grp * 4 + b
                pt, bp = next_psf()
                for k in range(16):
                    mm(pt[:], hTt[:, k, b * 128:(b + 1) * 128], wv[:, k, :], k == 0, k == 15,
                       [b_wv, bhT[b]], [bp])
                si = nst[0] % 2
                nst[0] += 1
                copy("dve" if b % 2 == 0 else "act", kst[si][:], pt[:], [bp], [b_kst[si]])
                P.dma("sp", V_d[blk * 128:(blk + 1) * 128, :], kst[si][:],
                      reads=[b_kst[si]], writes=[b_vscr[grp]], shared=True)

        if ngrp:
            p1_stats(0)
            p1_transposes(0, range(4))
        for grp in range(ngrp):
            nxt = grp + 1 < ngrp
            if nxt:
                p1_stats(grp + 1)
            p1_kT(grp)
            if nxt:
                p1_transposes(grp + 1, [0, 1])
            p1_ki(grp)
            p1_V(grp, [0, 1])
            if nxt:
                p1_transposes(grp + 1, [2, 3])
            p1_V(grp, [2, 3])
        dump("kiT2", kiT2[:], [128, SEQ], BF16, b_kiT2)

    P.barrier()
    nsb_run = nsb if stage >= 2 else 0
    for sbi in range(nsb_run):
        with ExitStack() as sS:
            U = sb("U%d" % sbi, [128, 8192], F32, sS)
            b_U = [Buf("U_%d" % i) for i in range(4)]
            qT = U[:, 0:4096].bitcast(BF16).rearrange("p (h t) -> p h t", h=16)
            gaz = U[:, 4096:8192].bitcast(BF16)
            gaT = gaz.rearrange("p (h t) -> p h t", h=16)
            x2 = U[:, :].rearrange("p (b c) -> p b c", b=4)
            bq = lambda h: b_U[h // 8]
            bgz = lambda q: b_U[2 + q // 2]
            M = sb("M%d" % sbi, [128, 16, 512], BF16, sS)
            b_M = [[Buf("M_%d_%d" % (g, lb)) for lb in range(4)] for g in range(16)]
            qiT2 = sb("qiT2_%d" % sbi, [128, 4, 512], BF16, sS)
            b_qi = [Buf("qi%d" % p) for p in range(4)]
            wi = sb("wi%d" % sbi, [128, 4, 8], F32, sS)
            b_wi = [Buf("wi%d" % lb) for lb in range(4)]
            own_blk = [4 * (4 * sbi + lb) + 3 for lb in range(4)]

            with ExitStack() as sA:
                g1bc = sb("g1bcA%d" % sbi, [128, D], F32, sA)
                b_g1bc = Buf("g1bcA")
                P.dma("sp", g1bc[:], g1_d.partition_broadcast(128), writes=[b_g1bc])
                hTa = sb("hTa%d" % sbi, [128, 16, 512], BF16, sA)
                b_hTa = [Buf("hTa%d" % lb) for lb in range(4)]
                xb = [sb("xbA%d_%d" % (sbi, i), [128, D], F32, sA) for i in range(2)]
                b_xb = [Buf("xbA%d" % i) for i in range(2)]
                hb = [sb("hbA%d_%d" % (sbi, i), [128, D], BF16, sA) for i in range(2)]
                b_hb = [Buf("hbA%d" % i) for i in range(2)]
                wt = [sb("wtA%d_%d" % (sbi, i), [128, 16, 512], BF16, sA) for i in range(2)]
                b_wt = [Buf("wtA%d" % i) for i in range(2)]
                wwi = sb("wwi%d" % sbi, [128, 16, 8], BF16, sA)
                b_wwi = Buf("wwi")
                t1 = [sb("t1_%d_%d" % (sbi, i), [128, 512], F32, sA) for i in range(2)]
                b_t1 = [Buf("t1_%d" % i) for i in range(2)]
                t2 = [sb("t2_%d_%d" % (sbi, i), [128, 512], F32, sA) for i in range(2)]
                b_t2 = [Buf("t2_%d" % i) for i in range(2)]
                stA = sb("stA%d" % sbi, [128, 2, 4, 4], F32, sA)
                b_stA = Buf("stA")
                st2 = sb("st2_%d" % sbi, [128, 4, 4], F32, sA)
                b_st2 = Buf("st2")


                tiles = ([("vs", c, C_VS + c * 512) for c in range(4)] +
                         [("q", c, C_Q + c * 512) for c in range(4)] +
                         [("u", c, C_U + c * 512) for c in range(4)] +
                         [("gb", c, C_GB + c * 512) for c in range(4)] +
                         [("ga", c, C_GA + c * 512) for c in range(4)] +
                         [("qi", 0, C_QI)])
                memset("dve", stA[:], 0.0, [b_stA])

                def issue(ti):
                    kind, c, col = tiles[ti]
                    wtile_dma(wt[ti % 2][:], w_in[:, col:col + 512], [b_wt[ti % 2]])

                gcount = [0]

                def gelu_core(pt, bp):
                    i = gcount[0] % 2
                    gcount[0] += 1
                    act(t1[i][:], pt[:], AF.Square, [bp], [b_t1[i]])
                    ts("dve", t1[i][:], t1[i][:], GA_, 1.0, ALU.mult, ALU.add, [b_t1[i]], [b_t1[i]])
                    tt("dve", t1[i][:], t1[i][:], pt[:], ALU.mult, [b_t1[i], bp], [b_t1[i]])
                    act(t2[i][:], t1[i][:], AF.Tanh, [b_t1[i]], [b_t2[i]], scale=GC)
                    return t2[i], b_t2[i]

                need_mix = []

                def do_mixing():
                    for lb in range(4):
                        for q in range(4):
                            pt, bp = next_psf()
                            for gg in range(4):
                                g = q * 4 + gg
                                mm(pt[:, gg * 128:(gg + 1) * 128],
                                   gaz[:, lb * 2048 + g * 128: lb * 2048 + (g + 1) * 128],
                                   wsT[:, g * 128:(g + 1) * 128], True, True,
                                   [bgz(lb), b_wsT], [bp], inc=(gg == 3))
                            for gg in range(4):
                                g = q * 4 + gg
                                stt("dve", M[:, g, lb * 128:(lb + 1) * 128],
                                    pt[:, gg * 128:(gg + 1) * 128], lngc[:, g:g + 1],
                                    bmix[:, g * 128:(g + 1) * 128], ALU.mult, ALU.add,
                                    [bp, b_cst, b_bmix], [b_M[g][lb]])
                    if sbi == 0:
                        dump("mixT", M[:], [128, 16, 512], BF16, [b for r_ in b_M for b in r_])

                issue(0)
                issue(1)
                for lb in range(4):
                    blk = own_blk[lb]
                    xi = lb % 2
                    P.dma("sp", xb[xi][:], x_sh[blk * 128:(blk + 1) * 128, :], writes=[b_xb[xi]])
                    norm_transpose(xb[xi][:], b_xb[xi], g1bc[:], b_g1bc, hb[xi][:], b_hb[xi],
                                   hTa[:, :, lb * 128:(lb + 1) * 128], [b_hTa[lb]] * 16, lb)
                P.dma("pool", wwi[:], w_in[:, C_WI:C_WI + 8].rearrange("(k p) c -> p k c", p=128),
                      writes=[b_wwi])
                for ti, (kind, c, col) in enumerate(tiles):
                    if ti >= 1 and ti + 1 < len(tiles):
                        issue(ti + 1)
                    w, bw = wt[ti % 2], b_wt[ti % 2]
                    if kind == "vs":
                        for lb in range(4):
                            pt, bp = next_psf()
                            for k in range(16):
                                mm(pt[:], hTa[:, k, lb * 128:(lb + 1) * 128], w[:, k, :],
                                   k == 0, k == 15, [b_hTa[lb], bw], [bp])
                            tz, btz = gelu_core(pt, bp)
                            zdst = gaz[:, lb * 2048 + c * 512: lb * 2048 + (c + 1) * 512]
                            stt("dve", zdst, tz[:], 1.0, pt[:], ALU.add, ALU.mult,
                                [btz, bp], [bgz(lb)])
                            act(junk[:, 0:512], zdst, AF.Identity, [bgz(lb), b_stA],
                                [b_junk, b_stA], accum=stA[:, 0, lb, c:c + 1])
                            act(junk[:, 512:1024], zdst, AF.Square, [bgz(lb), b_stA],
                                [b_junk, b_stA], accum=stA[:, 1, lb, c:c + 1])
                        if c == 3:
                            for lb in range(4):
                                mu, var, tmp = (st2[:, lb, 0:1], st2[:, lb, 1:2], st2[:, lb, 2:3])
                                reduce(mu, stA[:, 0, lb, :], ALU.add, [b_stA], [b_st2])
                                reduce(var, stA[:, 1, lb, :], ALU.add, [b_stA], [b_st2])
                                ts("dve", mu, mu, 1.0 / D, None, ALU.mult, None, [b_st2], [b_st2])
                                tt("dve", tmp, mu, mu, ALU.mult, [b_st2], [b_st2])
                                stt("dve", var, var, 1.0 / D, tmp, ALU.mult, ALU.subtract,
                                    [b_st2], [b_st2])
                                ts("dve", var, var, 4.0 * EPS, None, ALU.add, None, [b_st2], [b_st2])
                                tt("pool", var, var, nhalf[:, 0:1], ALU.pow, [b_st2, b_nhalf], [b_st2])
                                zb = gaz[:, lb * 2048:(lb + 1) * 2048]
                                ts("dve", zb, zb, mu, var, ALU.subtract, ALU.mult,
                                   [b_st2, bgz(lb)], [bgz(lb)])
                            need_mix.append(1)
                    else:
                        ncc = 4
                        if kind == "u" and need_mix:
                            need_mix.pop()
                            do_mixing()
                        for cc in range(ncc):
                            g = c * 4 + cc
                            pt, bp = next_psf()
                            for k in range(16):
                                mm(pt[:], w[:, k, cc * 128:(cc + 1) * 128], hTa[:, k, :],
                                   k == 0, k == 15, b_hTa + [bw], [bp])
                            if kind == "u":
                                tz, btz = gelu_core(pt, bp)
                                stt("dve", tz[:], tz[:], 1.0, pt[:], ALU.add, ALU.mult,
                                    [btz, bp], [btz])
                                stt("dve", M[:, g, :], tz[:], 0.25, M[:, g, :], ALU.mult, ALU.mult,
                                    [btz] + b_M[g], b_M[g])
                            elif kind == "gb":
                                i = gcount[0] % 2
                                gcount[0] += 1
                                act(t2[i][:], pt[:], AF.Tanh, [bp], [b_t2[i]], scale=0.5)
                                stt("dve", M[:, g, :], t2[i][:], 1.0, M[:, g, :], ALU.add, ALU.mult,
                                    [b_t2[i]] + b_M[g], b_M[g])
                            elif kind == "ga":
                                act(gaT[:, g, :], pt[:], AF.Tanh, [bp], [bgz(g // 4)], scale=0.5)
                            elif kind == "q":
                                copy("act" if cc % 2 == 0 else "dve", qT[:, g, :], pt[:], [bp], [bq(g)])
                            elif kind == "qi":
                                copy("act" if cc % 2 == 0 else "dve", qiT2[:, cc, :], pt[:], [bp],
                                     [b_qi[cc]])
                for lb in range(4):
                    pt, bp = next_psf()
                    for k in range(16):
                        mm(pt[:, 0:8], hTa[:, k, lb * 128:(lb + 1) * 128], wwi[:, k, :],
                           k == 0, k == 15, [b_hTa[lb], b_wwi], [bp])
                    copy("dve", wi[:, lb, :], pt[:, 0:8], [bp], [b_wi[lb]])
                if sbi == 0:
                    dump("sgT", M[:], [128, 16, 512], BF16, [b for r_ in b_M for b in r_])
                    dump("U", U[:], [128, 8192], F32, b_U)
                    dump("qiT2", qiT2[:], [128, 4, 512], BF16, b_qi)
                    dump("wi", wi[:], [128, 4, 8], F32, b_wi)

            P.barrier()
            if stage < 3:
                P.barrier()
                continue
            with ExitStack() as sB:
                S = sb("S%d" % sbi, [128, SEQ], F32, sB)
                b_S = Buf("S")
                mask = sb("mask%d" % sbi, [128, SEQ], BF16, sB)
                b_mask = Buf("mask")
                maskT = sb("maskT%d" % sbi, [128, 32, 128], BF16, sB)
                b_maskT = Buf("maskT")
                R = [sb("R%d_%d" % (sbi, h), [128, 512], BF16, sB) for h in range(8)]
                b_R = [Buf("R%d" % h) for h in range(8)]
                Dw = sb("Dw%d" % sbi, [128, 8, 128], BF16, sB)
                b_Dw = Buf("Dw")
                kTg = [sb("kTg%d_%d" % (sbi, i), [128, SEQ], BF16, sB) for i in range(2)]
                b_kTg = [Buf("kTg%d" % i) for i in range(2)]
                Vt = [sb("Vt%d_%d" % (sbi, i), [128, 32, 129], BF16, sB) for i in range(2)]
                b_Vt = [Buf("Vt%d" % i) for i in range(2)]
                for i in range(2):
                    memset("dve", Vt[i][:, :, 128:129], 1.0, [b_Vt[i]])
                Et = [sb("Et%d_%d" % (sbi, i), [128, 4, 128], BF16, sB) for i in range(3)]
                b_Et = [Buf("Et%d" % i) for i in range(3)]
                PTt = [sb("PT%d_%d" % (sbi, i), [128, 4, 128], BF16, sB) for i in range(3)]
                b_PT = [Buf("PT%d" % i) for i in range(3)]
                atok = sb("atok%d" % sbi, [128, D], BF16, sB)
                b_atok = Buf("atok")
                bs = sb("bs%d" % sbi, [128, 8], F32, sB)
                b_bs = Buf("bs")
                steps = sb("steps%d" % sbi, [128, NBIS + 1], F32, sB)
                b_steps = Buf("steps")
                rden = sb("rden%d" % sbi, [128, 4], F32, sB)
                b_rden = Buf("rden")
                mtmp = sb("mtmp%d" % sbi, [128, 8, 128], F32, sB)
                b_mtmp = Buf("mtmp")

                maskT2 = [maskT, sb("maskTb%d" % sbi, [128, 32, 128], BF16, sB)]
                b_maskT2 = [b_maskT, Buf("maskTb")]
                S2 = [S, sb("Sb%d" % sbi, [128, SEQ], F32, sB)]
                b_S2 = [b_S, Buf("Sb")]
                den = sb("den%d" % sbi, [128, 16], F32, sB)
                b_den = Buf("den")
                PS_ACC = (psb[1][0][:].rearrange("p a b -> p (a b)").bitcast(F32), psb[1][1])
                PS_TR = psb[0]

                def F1(lb):
                    m = 4 * sbi + lb
                    nkt = m + 1
                    tok = slice(lb * 128, (lb + 1) * 128)
                    Sx, bSx = S2[lb % 2], b_S2[lb % 2]
                    for h in range(8):
                        ts("dve", Dw[:, h, :], ident[:], wi[:, lb, h:h + 1], None, ALU.mult, None,
                           [b_ident, b_wi[lb]], [b_Dw])
                    yield
                    for kt in range(nkt):
                        ks = slice(kt * 512, (kt + 1) * 512)
                        for hp in range(4):
                            for h in (2 * hp, 2 * hp + 1):
                                rows = slice(64 * (h % 2), 64 * (h % 2) + 64)
                                pt, bp = psf[4 + h % 2]
                                mm(pt[:], qiT2[rows, h // 2, tok], kiT2[rows, ks], True, True,
                                   [b_qi[h // 2], b_kiT2[kt]], [bp])
                            for h in (2 * hp, 2 * hp + 1):
                                pt, bp = psf[4 + h % 2]
                                act(R[h][:], pt[:], AF.Relu, [bp], [b_R[h]])
                            yield
                            yield
                        pa, bpa = PS_ACC
                        for h in range(8):
                            mm(pa, Dw[:, h, :], R[h][:], h == 0, h == 7, [b_Dw, b_R[h]], [bpa])
                            if h == 7:
                                copy("act", Sx[:, ks], pa, [bpa], [bSx])
                            yield

                def F2(lb):
                    m = 4 * sbi + lb
                    nkb = 4 * m + 4
                    L = nkb * 128
                    Sx, bSx = S2[lb % 2], b_S2[lb % 2]
                    mT, b_mT = maskT2[lb % 2], b_maskT2[lb % 2]
                    rmax, lo, wid, mid, cnt, sgn = (bs[:, i:i + 1] for i in range(6))
                    reduce(rmax, Sx[:, 0:L], ALU.max, [bSx], [b_bs])
                    reduce(lo, Sx[:, 0:L], ALU.min, [bSx], [b_bs])
                    yield
                    tt("dve", wid, rmax, lo, ALU.subtract, [b_bs], [b_bs])
                    stt("dve", lo, wid, -0.02, lo, ALU.mult, ALU.add, [b_bs], [b_bs])
                    ts("dve", wid, wid, 1.04, None, ALU.mult, None, [b_bs], [b_bs])
                    ts("dve", steps[:], pow2[:], wid, None, ALU.mult, None, [b_cst, b_bs], [b_steps])
                    tt("dve", mid, lo, steps[:, 0:1], ALU.add, [b_bs, b_steps], [b_bs])
                    tt("dve", Sx[:, 0:512], Sx[:, 0:512], fake[:], ALU.add, [bSx, b_cst], [bSx])
                    tt("dve", Sx[:, L - 128:L], Sx[:, L - 128:L], negm[:], ALU.add, [bSx, b_cst], [bSx])
                    yield
                    for j in range(NBIS):
                        ts("dve", mask[:, 0:L], Sx[:, 0:L], mid, 0.0, ALU.is_ge, ALU.add,
                           [bSx, b_bs], [b_mask, b_bs], accum=cnt)
                        ts("dve", sgn, cnt, TOPK - 0.5, None, ALU.is_ge, None, [b_bs], [b_bs])
                        stt("dve", lo, sgn, steps[:, j:j + 1], lo, ALU.mult, ALU.add,
                            [b_bs, b_steps], [b_bs])
                        tt("dve", mid, lo, steps[:, j + 1:j + 2], ALU.add, [b_bs, b_steps], [b_bs])
                        yield
                    ts("dve", mask[:, 0:L], Sx[:, 0:L], lo, -BIGM, ALU.is_lt, ALU.mult,
                       [bSx, b_bs], [b_mask])
                    yield
                    for jb in range(0, nkb, 8):
                        nj = min(8, nkb - jb)
                        pt, bp = PS_TR
                        for jj in range(nj):
                            j = jb + jj
                            P.op("pe", lambda e, o=pt[:, jj, :], i=mask[:, j * 128:(j + 1) * 128]:
                                 e.transpose(o, i, ident[:]), [b_mask, b_ident], [bp],
                                 inc=(jj == nj - 1))
                        copy("dve", mT[:, jb:jb + nj, :], pt[:, 0:nj, :], [bp], [b_mT])
                        yield
                    if sbi == 0 and lb == 1:
                        dump("S", Sx[:, 0:L], [128, L], F32, [bSx])
                        dump("lo", bs[:, 0:6], [128, 6], F32, [b_bs])

                def back(lb):
                    m = 4 * sbi + lb
                    nkb = 4 * m + 4
                    nkt = m + 1
                    L = nkb * 128
                    tok = slice(lb * 128, (lb + 1) * 128)
                    mT, b_mT = maskT2[lb % 2], b_maskT2[lb % 2]
                    for g in range(4):
                        slot = (lb * 4 + g) % 2
                        P.dma("sp", kTg[slot][:, 0:L], kT_d[g, :, 0:L],
                              reads=b_kscr[0:nkt], writes=[b_kTg[slot]])
                        P.dma("sp", Vt[slot][:, 0:nkb, 0:128],
                              V_d[0:L, g * 128:(g + 1) * 128].rearrange("(j s) d -> s j d", s=128),
                              reads=b_vscr[0:nkt], writes=[b_Vt[slot]])
                        po = [psf[2], psf[3]]
                        lgs = {}

                        def qk(j):
                            lg, blg = psf[rot["a"] % 2]
                            rot["a"] += 1
                            near = j >= nkb - 2
                            lg3 = lg[:].rearrange("p (r t) -> p r t", r=4)
                            mm(lg3, kTg[slot][:, j * 128:(j + 1) * 128],
                               qT[:, 4 * g:4 * g + 4, tok], True, False,
                               [b_kTg[slot], bq(4 * g)], [blg])
                            mm(lg3, ident[:],
                               mT[:, j, :].unsqueeze(1).to_broadcast([128, 4, 128]), False, not near,
                               [b_ident, b_mT], [blg])
                            if near:
                                kk = nkb - 1 - j
                                mm(lg3, ident[:], biasT[:, kk, 4 * g:4 * g + 4, :], False, True,
                                   [b_ident, b_biasT], [blg])
                            lgs[j] = (lg, blg)

                        qk(0)
                        for j in range(nkb):
                            if j + 1 < nkb:
                                qk(j + 1)
                            lg, blg = lgs.pop(j)
                            e = j % 3
                            act(PTt[e][:], lg[:].rearrange("p (r t) -> p r t", r=4), AF.Exp,
                                [blg], [b_PT[e]], scale=SCALE)
                            for r in range(4):
                                pb, bpb = po[r // 2]
                                c0 = (r % 2) * 129
                                mm(pb[:, c0:c0 + 129], PTt[e][:, r, :], Vt[slot][:, j, :],
                                   (j == 0 and r % 2 == 0), (j == nkb - 1 and r % 2 == 1),
                                   [b_PT[e], b_Vt[slot]], [bpb],
                                   inc=(r == 3 or j == nkb - 1))
                            yield
                        for half in range(2):
                            pb, bpb = po[half]
                            v3 = pb[:, 0:258].rearrange("p (r c) -> p r c", c=129)
                            h0 = 4 * g + 2 * half
                            copy("act", atok[:, h0 * 128:(h0 + 2) * 128].rearrange("p (r c) -> p r c", r=2),
                                 v3[:, :, 0:128], [bpb], [b_atok])
                            copy("act", den[:, h0:h0 + 2], v3[:, :, 128], [bpb], [b_den])
                        yield
                    P.op("dve", lambda e: e.reciprocal(den[:], den[:]), [b_den], [b_den])
                    a3 = atok[:].rearrange("p (h c) -> p h c", h=16)
                    tt("dve", a3, a3, den[:, :].unsqueeze(2).to_broadcast([128, 16, 128]), ALU.mult,
                       [b_atok, b_den], [b_atok])
                    if sbi == 0 and lb == 1:
                        dump("atok", atok[:], [128, D], BF16, [b_atok])
                    for hh in range(2):
                        pt, bp = PS_TR
                        for jj in range(8):
                            h = hh * 8 + jj
                            P.op("pe", lambda e, o=pt[:, jj, :], i=atok[:, h * 128:(h + 1) * 128]:
                                 e.transpose(o, i, ident[:]), [b_atok, b_ident], [bp], inc=(jj == 7))
                        hs = slice(hh * 8, hh * 8 + 8)
                        stt("dve", mtmp[:], gaT[:, hs, tok], 1.0, pt[:], ALU.add, ALU.mult,
                            [bgz(2 * hh), bgz(2 * hh + 1), bp], [b_mtmp])
                        mb = [b_M[h][lb] for h in range(hh * 8, hh * 8 + 8)]
                        stt("dve", M[:, hs, tok], mtmp[:], 0.5, M[:, hs, tok], ALU.mult, ALU.add,
                            [b_mtmp] + mb, mb)
                        yield

                def len_F1(lb):
                    return 1 + 16 * (4 * sbi + lb + 1)

                def len_F2(lb):
                    return NBIS + 4 + (4 * (4 * sbi + lb) + 4 + 7) // 8

                def len_back(lb):
                    return 4 * (4 * (4 * sbi + lb) + 4) + 4 + 2

                def run_par(gens):
                    master, nm = gens[0]
                    slaves = [[g, n, 0] for g, n in gens[1:]]
                    done = 0
                    for _ in master:
                        done += 1
                        for sl in slaves:
                            while sl[0] is not None and sl[2] * nm < done * sl[1]:
                                try:
                                    next(sl[0])
                                    sl[2] += 1
                                except StopIteration:
                                    sl[0] = None
                    for sl in slaves:
                        if sl[0] is not None:
                            for _ in sl[0]:
                                pass

                run_par([(F1(0), len_F1(0))])
                run_par([(F2(0), len_F2(0)), (F1(1), len_F1(1))])
                for lb in range(4):
                    gl = [(back(lb), len_back(lb))]
                    if lb + 1 < 4:
                        gl.append((F2(lb + 1), len_F2(lb + 1)))
                    if lb + 2 < 4:
                        gl.append((F1(lb + 2), len_F1(lb + 2)))
                    run_par(gl)
                if sbi == 0:
                    dump("merged", M[:], [128, 16, 512], BF16, [b for r_ in b_M for b in r_])

            P.barrier()
            if stage < 4:
                P.barrier()
                continue
            with ExitStack() as sC:
                wt = [sb("wtC%d_%d" % (sbi, i), [128, 16, 512], BF16, sC) for i in range(4)]
                b_wt = [Buf("wtC%d" % i) for i in range(4)]
                g2bc = sb("g2bc%d" % sbi, [128, D], F32, sC)
                b_g2bc = Buf("g2bc")
                hbC = [sb("hbC%d_%d" % (sbi, i), [128, D], BF16, sC) for i in range(2)]
                b_hbC = [Buf("hbC%d" % i) for i in range(2)]
                for c in range(4):
                    wtile_dma(wt[c][:], w_out[:, c * 512:(c + 1) * 512], [b_wt[c]])
                P.dma("sp", g2bc[:], g2_d.partition_broadcast(128), writes=[b_g2bc])
                for lb in range(4):
                    blk = own_blk[lb]
                    P.dma("sp", x2[:, lb, :], x_sh[blk * 128:(blk + 1) * 128, :], writes=[b_U[lb]])

                def c_stats(lb):
                    col = ss[:, lb:lb + 1]
                    bsc = b_ssc[lb]
                    jk, bjk = (junk, b_junk) if lb % 2 == 0 else (junkb, b_junkb)
                    act(jk[:], x2[:, lb, :], AF.Square, [b_U[lb]], [bjk, bsc], accum=col)
                    ts("dve", col, col, 1.0 / D, EPS, ALU.mult, ALU.add, [bsc], [bsc])
                    tt("pool", col, col, nhalf[:, 0:1], ALU.pow, [bsc, b_nhalf], [bsc])
                    stt("dve", hbC[lb % 2][:], x2[:, lb, :], col, g2bc[:], ALU.mult, ALU.mult,
                        [b_U[lb], bsc, b_g2bc], [b_hbC[lb % 2]])

                def c_transposes(lb):
                    for half in range(2):
                        pt, bp = next_psb()
                        for kk in range(8):
                            k = half * 8 + kk
                            P.op("pe", lambda e, o=pt[:, kk, :], i=hbC[lb % 2][:, k * 128:(k + 1) * 128]:
                                 e.transpose(o, i, ident[:]), [b_hbC[lb % 2], b_ident], [bp], inc=(kk == 7))
                        copy("act" if half == 0 else "dve",
                             M[:, half * 8:(half + 1) * 8, lb * 128:(lb + 1) * 128], pt[:], [bp],
                             [b_M[k][lb] for k in range(half * 8, half * 8 + 8)])

                for lb in range(4):
                    for c in range(4):
                        pt, bp = next_psf()
                        for k in range(16):
                            mm(pt[:], M[:, k, lb * 128:(lb + 1) * 128], wt[c][:, k, :], k == 0, k == 15,
                               [b_M[k][lb], b_wt[c]], [bp])
                        xs = x2[:, lb, c * 512:(c + 1) * 512]
                        tt("dve", xs, xs, pt[:], ALU.add, [b_U[lb], bp], [b_U[lb]])
                    if sbi == 0 and lb == 3:
                        dump("x2", U[:], [128, 8192], F32, b_U)
                    c_stats(lb)
                    if lb >= 1:
                        c_transposes(lb - 1)
                c_transposes(3)

            if stage < 5:
                P.barrier()
                continue
            with ExitStack() as sD:
                ffT = sb("ffT%d" % sbi, [128, 16, 512], BF16, sD)
                b_ffT = [Buf("ffT%d" % i) for i in range(16)]
                wt = [sb("wtD%d_%d" % (sbi, i), [128, 16, 512], BF16, sD) for i in range(2)]
                b_wt = [Buf("wtD%d" % i) for i in range(2)]
                rt = [sb("rt%d_%d" % (sbi, i), [128, 512], F32, sD) for i in range(2)]
                b_rt = [Buf("rt%d" % i) for i in range(2)]
                h2T = M
                b_h2all = [b for r_ in b_M for b in r_]
                srcs = []
                for seg in range(4):
                    for c in range(4):
                        srcs.append(("w1", seg, c, w_ff1[:, seg * 2048 + c * 512: seg * 2048 + (c + 1) * 512]))
                    for c in range(4):
                        srcs.append(("w2", seg, c, w_ff2[seg * 2048:(seg + 1) * 2048, c * 512:(c + 1) * 512]))
                wtile_dma(wt[0][:], srcs[0][3], [b_wt[0]])
                nrt = 0
                for ti, (kind, seg, c, src) in enumerate(srcs):
                    if ti + 1 < len(srcs):
                        wtile_dma(wt[(ti + 1) % 2][:], srcs[ti + 1][3], [b_wt[(ti + 1) % 2]])
                    w, bw = wt[ti % 2], b_wt[ti % 2]
                    if kind == "w1":
                        for cc in range(4):
                            fc = c * 4 + cc
                            pt, bp = next_psf()
                            for k in range(16):
                                mm(pt[:], w[:, k, cc * 128:(cc + 1) * 128], h2T[:, k, :], k == 0, k == 15,
                                   [b_M[k][lb] for lb in range(4)] + [bw], [bp])
                            i = nrt % 2
                            nrt += 1
                            act(rt[i][:], pt[:], AF.Relu, [bp], [b_rt[i]])
                            tt("dve", ffT[:, fc, :], rt[i][:], rt[i][:], ALU.mult, [b_rt[i]], [b_ffT[fc]])
                    else:
                        for lb in range(4):
                            pt, bp = next_psf()
                            for fc in range(16):
                                mm(pt[:], ffT[:, fc, lb * 128:(lb + 1) * 128], w[:, fc, :],
                                   fc == 0, fc == 15, [b_ffT[fc], bw], [bp])
                            xs = x2[:, lb, c * 512:(c + 1) * 512]
                            tt("dve", xs, xs, pt[:], ALU.add, [b_U[lb], bp], [b_U[lb]])
                if sbi == 0:
                    dump("x3", U[:], [128, 8192], F32, b_U)

            P.barrier()
            with ExitStack() as sE:
                fgbc = sb("fgbc%d" % sbi, [128, D], F32, sE)
                b_fgbc = Buf("fgbc")
                P.dma("sp", fgbc[:], fg_d.partition_broadcast(128), writes=[b_fgbc])
                ot = [sb("ot%d_%d" % (sbi, i), [128, D], F32, sE) for i in range(2)]
                b_ot = [Buf("ot%d" % i) for i in range(2)]
                for lb in range(4):
                    col = ss[:, lb:lb + 1]
                    bsc = b_ssc[lb]
                    jk, bjk = (junk, b_junk) if lb % 2 == 0 else (junkb, b_junkb)
                    act(jk[:], x2[:, lb, :], AF.Square, [b_U[lb]], [bjk, bsc], accum=col)
                    ts("dve", col, col, 1.0 / D, EPS, ALU.mult, ALU.add, [bsc], [bsc])
                    tt("pool", col, col, nhalf[:, 0:1], ALU.pow, [bsc, b_nhalf], [bsc])
                    o, bo = ot[lb % 2], b_ot[lb % 2]
                    stt("dve", o[:], x2[:, lb, :], col, fgbc[:], ALU.mult, ALU.mult,
                        [b_U[lb], bsc, b_fgbc], [bo])
                    row = (4 * sbi + lb) * 128
                    P.dma("sp", y_out[row:row + 128, :], o[:], reads=[bo])

            P.barrier()
        P.barrier()
    P.finish()
    P.emit()
    es.close()
    return nc


def _rel_bucket_np(dist):
    n = np.maximum(dist, 0)
    max_exact = 16
    nf = np.maximum(n, 1).astype(np.float32)
    large = max_exact + (np.log(nf / max_exact) / np.log(128 / max_exact) * (32 - max_exact)).astype(np.int32)
    large = np.minimum(large, 31)
    return np.where(n < max_exact, n, large)


def make_in_maps(inputs):
    x = np.asarray(inputs["x"], np.float32)
    rel_bias = np.asarray(inputs["rel_bias"], np.float32)
    f = lambda k: np.ascontiguousarray(np.asarray(inputs[k], np.float32))
    w_in = f("w_in")[0]
    w_out = f("w_out")[0]
    w_ff1 = f("w_ff1")[0]
    w_ff2 = f("w_ff2")[0]
    s_idx = np.arange(128)[:, None]
    t_idx = np.arange(128)[None, :]
    bidx = np.stack([_rel_bucket_np(t_idx - s_idx + 128 * k) for k in range(2)], axis=1)
    braw = np.ascontiguousarray(rel_bias[bidx].transpose(0, 1, 3, 2)).astype(np.float32)
    cfar = np.ascontiguousarray(np.broadcast_to(rel_bias[31][None, :], (128, 16))).astype(np.float32)
    common = {
        "w_in": w_in, "w_out": w_out, "w_ff1": w_ff1, "w_ff2": w_ff2,
        "norm1_g": f("norm1_g").reshape(1, D), "norm2_g": f("norm2_g").reshape(1, D),
        "final_g": f("final_g").reshape(1, D),
        "lng_col": np.ascontiguousarray(f("sgu_ln_g").reshape(16, 128).T),
        "lnb_col": np.ascontiguousarray(f("sgu_ln_b").reshape(16, 128).T),
        "sgub_row": f("sgu_b").reshape(1, D),
        "wsT": np.ascontiguousarray(f("sgu_w")[0].transpose(2, 0, 1)).reshape(128, 2048),
        "trilT": (s_idx <= t_idx).astype(np.float32),
        "ident_h": np.eye(128, dtype=np.float32).astype(ml_dtypes.bfloat16),
        "negmask": np.where(np.arange(128)[None, :] <= np.arange(128)[:, None], 0.0, NEG).astype(np.float32),
        "braw": braw.reshape(128, 4096), "cfar": cfar,
        "pow2": np.ascontiguousarray(np.broadcast_to(
            (0.5 ** np.arange(1, NBIS + 2, dtype=np.float64))[None, :], (128, NBIS + 1))).astype(np.float32),
    }
    in_maps = []
    for c in range(8):
        b, r = c // 4, c % 4
        pad = 3 - r
        xs = np.zeros((SEQ, D), np.float32)
        n = SEQ - pad * 128
        xs[pad * 128:] = x[b, :n]
        fake = np.zeros((128, 512), np.float32)
        fake[:, :pad * 128] = NEG
        m = dict(common)
        m["x_sh"] = xs
        m["fakemask"] = fake
        in_maps.append(m)
    return in_maps


def kernel(**inputs):
    in_maps = make_in_maps(inputs)
    nc = build_program()
    res = run_bass_kernel_spmd(nc, in_maps, core_ids=list(range(8)))
    out = np.zeros((2, SEQ, D), np.float32)
    for c in range(8):
        b, r = c // 4, c % 4
        y = np.asarray(res.results[c]["y"], np.float32)
        for m in range(8):
            blk = 4 * m + r
            out[b, blk * 128:(blk + 1) * 128] = y[m * 128:(m + 1) * 128]
    return out
```

```python
import os
from contextlib import ExitStack

import numpy as np
import ml_dtypes

import concourse.bass as bass
import concourse.mybir as mybir
from concourse.bass_utils import run_bass_kernel_spmd

F32, BF16 = mybir.dt.float32, mybir.dt.bfloat16
ALU, AF, AX = mybir.AluOpType, mybir.ActivationFunctionType, mybir.AxisListType

D = 2048
SEQ = 4096
NBLK = 32
D_IN = 11848
C_Q, C_K, C_V, C_QI, C_KI, C_WI, C_U, C_VS, C_GA, C_GB = (
    0, 2048, 2560, 3072, 3584, 3648, 3656, 5704, 7752, 9800)
DFF = 8192
EPS = 1e-6
TOPK = 256
NBIS = 20
SCALE = 128 ** -0.5
GC = 0.7978845608028654
GA_ = 0.044715
NEG = -1.0e30
BIGM = 30000.0

COMPUTE = ("pe", "act", "dve", "pool")


class Buf:
    __slots__ = ("w", "r", "name")

    def __init__(self, name=""):
        self.w = []
        self.r = {}
        self.name = name


class Prog:
    def __init__(self, nc, es, ndsem=16):
        self.nc = nc
        self.streams = {e: [] for e in ("pe", "act", "dve", "pool", "sp")}
        self.cnt = {e: 0 for e in COMPUTE}
        self.waited = {e: {} for e in self.streams}
        self.sem = {}
        for e in COMPUTE:
            self.sem[e] = es.enter_context(nc.semaphore("s_" + e))
        self.K = ndsem
        self.dn = {"sp": 0, "pool": 0}
        for q in ("sp", "pool"):
            for k in range(ndsem):
                self.sem[(q, k)] = es.enter_context(nc.semaphore("d_%s%d" % (q, k)))

    def _deps(self, eng, reads, writes, shared=False):
        deps = {}
        for b in reads:
            for (k, v) in b.w:
                if deps.get(k, 0) < v:
                    deps[k] = v
        for b in writes:
            if not shared:
                for (k, v) in b.w:
                    if deps.get(k, 0) < v:
                        deps[k] = v
            for k, v in b.r.items():
                if deps.get(k, 0) < v:
                    deps[k] = v
        waits = []
        wd = self.waited[eng]
        for k, v in deps.items():
            if eng == "pe" and k == "pe":
                continue
            if wd.get(k, 0) >= v:
                continue
            wd[k] = v
            waits.append((k, v))
        return waits

    def op(self, eng, fn, reads=(), writes=(), inc=True):
        waits = self._deps(eng, reads, writes)
        if inc:
            self.cnt[eng] += 1
            tv = self.cnt[eng]
        else:
            tv = self.cnt[eng] + 1
        self.streams[eng].append((waits, fn, (eng, 1) if inc else None))
        for b in reads:
            if b.r.get(eng, 0) < tv:
                b.r[eng] = tv
        for b in writes:
            b.w = [(eng, tv)]
            b.r = {}

    def dma(self, q, out, in_, reads=(), writes=(), shared=False):
        waits = self._deps(q, reads, writes, shared)
        n = self.dn[q]
        self.dn[q] = n + 1
        k = n % self.K
        val = 16 * (n // self.K + 1)
        key = (q, k)
        if n >= self.K and self.waited[q].get(key, 0) < val - 16:
            self.waited[q][key] = val - 16
            waits.append((key, val - 16))
        self.streams[q].append(
            (waits, lambda e, o=out, i=in_: e.dma_start(out=o, in_=i), (key, 16)))
        for b in reads:
            if b.r.get(key, 0) < val:
                b.r[key] = val
        for b in writes:
            if shared:
                b.w = list(b.w) + [(key, val)]
            else:
                b.w = [(key, val)]
            b.r = {}

    def barrier(self):
        waits = []
        for q in ("sp", "pool"):
            n = self.dn[q]
            for k in range(min(n, self.K)):
                last = ((n - 1 - k) // self.K) * self.K + k
                waits.append(((q, k), 16 * (last // self.K + 1)))
        for e in COMPUTE:
            if self.cnt[e]:
                waits.append((e, self.cnt[e]))
        for e in self.streams:
            ws = []
            for k, v in waits:
                if e == "pe" and k == "pe":
                    continue
                if self.waited[e].get(k, 0) >= v:
                    continue
                self.waited[e][k] = v
                ws.append((k, v))
            if ws:
                self.streams[e].append((ws, None, None))

    def finish(self):
        waits = []
        for q in ("sp", "pool"):
            n = self.dn[q]
            for k in range(min(n, self.K)):
                last = ((n - 1 - k) // self.K) * self.K + k
                waits.append(((q, k), 16 * (last // self.K + 1)))
        for e in COMPUTE:
            if self.cnt[e]:
                waits.append((e, self.cnt[e]))
        self.streams["sp"].append((waits, None, None))

    def emit(self):
        nc = self.nc
        with nc.Block() as block:
            def mk(name):
                def body(eng):
                    for waits, fn, inc in self.streams[name]:
                        for k, v in waits:
                            eng.wait_ge(self.sem[k], v)
                        if fn is None:
                            continue
                        ins = fn(eng)
                        if inc is not None:
                            ins.then_inc(self.sem[inc[0]], inc[1])
                return body
            block.tensor(mk("pe"))
            block.scalar(mk("act"))
            block.vector(mk("dve"))
            block.gpsimd(mk("pool"))
            block.sync(mk("sp"))


def build_program(stage=99, debug=False, nsb=2):
    nc = bass.Bass("TRN2", target_bir_lowering=False)
    es = ExitStack()
    P = Prog(nc, es)

    def din(name, shape, dt=F32):
        return nc.dram_tensor(name, list(shape), dt, kind="ExternalInput").ap()

    def dout(name, shape, dt=F32):
        return nc.dram_tensor(name, list(shape), dt, kind="ExternalOutput").ap()

    def dscr(name, shape, dt):
        kind = "ExternalOutput" if debug else "Internal"
        return nc.dram_tensor(name, list(shape), dt, kind=kind).ap()

    def sb(name, shape, dt, stack=None):
        return (stack or es).enter_context(nc.sbuf_tensor("s_" + name, list(shape), dt))

    x_sh = din("x_sh", [SEQ, D])
    w_in = din("w_in", [D, D_IN])
    w_out = din("w_out", [D, D])
    w_ff1 = din("w_ff1", [D, DFF])
    w_ff2 = din("w_ff2", [DFF, D])
    g1_d = din("norm1_g", [1, D])
    g2_d = din("norm2_g", [1, D])
    fg_d = din("final_g", [1, D])
    lng_d = din("lng_col", [128, 16])
    lnb_d = din("lnb_col", [128, 16])
    sgub_d = din("sgub_row", [1, D])
    wsT_d = din("wsT", [128, 16 * 128])
    tril_d = din("trilT", [128, 128])
    ident_d = din("ident_h", [128, 128], BF16)
    negm_d = din("negmask", [128, 128])
    fake_d = din("fakemask", [128, 512])
    braw_d = din("braw", [128, 2 * 16 * 128])
    cfar_d = din("cfar", [128, 16])
    pow2_d = din("pow2", [128, NBIS + 1])
    y_out = dout("y", [1024, D])

    kT_d = dscr("kT_scr", [4, 128, SEQ], BF16)
    V_d = dscr("V_scr", [SEQ, 512], BF16)
    dbg = {}

    def dump(name, ap, shape, dt, reads):
        if not debug:
            return
        d = dout("dbg_" + name, shape, dt)
        P.dma("sp", d, ap, reads=reads)

    psf = []
    for i in range(6):
        t = es.enter_context(nc.psum_tensor("psf%d" % i, [128, 512], F32))
        psf.append((t, Buf("psf%d" % i)))
    psb = []
    for i in range(2):
        t = es.enter_context(nc.psum_tensor("psb%d" % i, [128, 8, 128], BF16))
        psb.append((t, Buf("psb%d" % i)))
    rot = {"f": 0, "b": 0, "s": 0, "a": 0}

    def next_psf():
        i = rot["f"] % 6
        rot["f"] += 1
        return psf[i]

    def next_ps4():
        i = rot["s"] % 4
        rot["s"] += 1
        return psf[i]

    def next_psb():
        i = rot["b"] % 2
        rot["b"] += 1
        return psb[i]

    def mm(out, lhsT, rhs, start, stop, reads, writes, inc=None):
        if inc is None:
            inc = stop
        P.op("pe", lambda e, o=out, l=lhsT, r=rhs, s=start, t=stop:
             e.matmul(o, l, r, start=s, stop=t), reads, writes, inc)

    def act(out, in_, func, reads, writes, bias=None, scale=None, accum=None):
        kw = {}
        if bias is not None:
            kw["bias"] = bias
        if scale is not None:
            kw["scale"] = scale
        if accum is not None:
            kw["accum_out"] = accum
        P.op("act", lambda e, o=out, i=in_, f=func, kw=kw: e.activation(o, i, f, **kw),
             reads, writes)

    def ts(eng, out, in0, s1, s2, op0, op1, reads, writes, accum=None):
        kw = {}
        if op1 is not None:
            kw["op1"] = op1
        if accum is not None:
            kw["accum_out"] = accum
        P.op(eng, lambda e, o=out, i=in0, a=s1, b=s2, p=op0, kw=kw:
             e.tensor_scalar(o, i, a, b, p, **kw), reads, writes)

    def stt(eng, out, in0, scalar, in1, op0, op1, reads, writes):
        P.op(eng, lambda e, o=out, i=in0, s=scalar, j=in1, p=op0, q=op1:
             e.scalar_tensor_tensor(o, i, s, j, p, q), reads, writes)

    def tt(eng, out, in0, in1, op, reads, writes):
        P.op(eng, lambda e, o=out, i=in0, j=in1, p=op: e.tensor_tensor(o, i, j, p),
             reads, writes)

    def copy(eng, out, in_, reads, writes):
        if eng == "act":
            P.op("act", lambda e, o=out, i=in_: e.copy(o, i), reads, writes)
        else:
            P.op(eng, lambda e, o=out, i=in_: e.tensor_copy(o, i), reads, writes)

    def memset(eng, ap, val, writes):
        P.op(eng, lambda e, o=ap, v=val: e.memset(o, v), (), writes)

    def reduce(out, in_, op, reads, writes):
        P.op("dve", lambda e, o=out, i=in_, p=op: e.tensor_reduce(o, i, AX.X, p), reads, writes)

    def wtile_dma(dst, src, writes):
        P.dma("pool", dst, src.rearrange("(k p) c -> p k c", p=128), writes=writes)

    ident = sb("ident", [128, 128], BF16)
    b_ident = Buf("ident")
    P.dma("sp", ident[:], ident_d, writes=[b_ident])
    kiT2 = sb("kiT2", [128, SEQ], BF16)
    b_kiT2 = [Buf("kiT2_%d" % g) for g in range(8)]
    ss = sb("ss", [128, 8], F32)
    b_ss = Buf("ss")
    nhalf = sb("nhalf", [128, 1], F32)
    b_nhalf = Buf("nhalf")
    P.op("pool", lambda e: e.memset(nhalf[:], -0.5), (), [b_nhalf])
    junk = sb("junk", [128, 2048], BF16)
    b_junk = Buf("junk")
    junkb = sb("junkb", [128, 2048], BF16)
    b_junkb = Buf("junkb")
    b_ssc = [Buf("ss%d" % i) for i in range(8)]
    negm = sb("negm", [128, 128], F32)
    fake = sb("fake", [128, 512], F32)
    pow2 = sb("pow2", [128, NBIS + 1], F32)
    lngc = sb("lngc", [128, 16], F32)
    lnbc = sb("lnbc", [128, 16], F32)
    b_cst = Buf("consts")
    P.dma("sp", negm[:], negm_d, writes=[b_cst], shared=True)
    P.dma("sp", fake[:], fake_d, writes=[b_cst], shared=True)
    P.dma("sp", pow2[:], pow2_d, writes=[b_cst], shared=True)
    P.dma("sp", lngc[:], lng_d, writes=[b_cst], shared=True)
    P.dma("sp", lnbc[:], lnb_d, writes=[b_cst], shared=True)
    ones_bf = sb("ones_bf", [128, 128], BF16)
    b_ones = Buf("ones")
    memset("dve", ones_bf[:], 1.0, [b_ones])
    biasT = sb("biasT", [128, 2, 16, 128], BF16)
    b_biasT = Buf("biasT")
    wsT = sb("wsT", [128, 16 * 128], BF16)
    b_wsT = Buf("wsT")
    bmix = sb("bmix", [128, 16 * 128], F32)
    b_bmix = Buf("bmix")

    if stage >= 2:
        with ExitStack() as s0:
            braw = sb("braw", [128, 2 * 16 * 128], F32, s0)
            cfar = sb("cfar", [128, 16], F32, s0)
            wsf = sb("wsf", [128, 16 * 128], F32, s0)
            tril = sb("tril", [128, 128], F32, s0)
            b_tmp = Buf("setup_tmp")
            P.dma("sp", braw[:], braw_d, writes=[b_tmp], shared=True)
            P.dma("sp", cfar[:], cfar_d, writes=[b_tmp], shared=True)
            P.dma("sp", wsf[:], wsT_d, writes=[b_tmp], shared=True)
            P.dma("sp", tril[:], tril_d, writes=[b_tmp], shared=True)
            P.dma("sp", bmix[:], sgub_d.partition_broadcast(128), writes=[b_bmix])
            b_tmp2 = Buf("setup_tmp2")
            for k in range(2):
                v = braw[:, k * 2048:(k + 1) * 2048].rearrange("p (h t) -> p h t", h=16)
                tt("dve", v, v, cfar[:, :].unsqueeze(2).to_broadcast([128, 16, 128]),
                   ALU.subtract, [b_tmp], [b_tmp2])
                ts("dve", biasT[:, k, :, :], v, 1.0 / SCALE, None, ALU.mult, None,
                   [b_tmp2], [b_biasT])
            tt("dve", wsT[:].rearrange("p (g t) -> p g t", g=16),
               wsf[:].rearrange("p (g t) -> p g t", g=16),
               tril[:, :].unsqueeze(1).to_broadcast([128, 16, 128]), ALU.mult,
               [b_tmp], [b_wsT])
            for q in range(4):
                pt, bp = next_psf()
                mm(pt[:], ones_bf[:], wsT[:, q * 512:(q + 1) * 512], True, True,
                   [b_ones, b_wsT], [bp])
                for gg in range(4):
                    g = q * 4 + gg
                    stt("dve", bmix[:, g * 128:(g + 1) * 128], pt[:, gg * 128:(gg + 1) * 128],
                        lnbc[:, g:g + 1], bmix[:, g * 128:(g + 1) * 128], ALU.mult, ALU.add,
                        [bp, b_cst, b_bmix], [b_bmix])
        P.barrier()
        dump("biasT", biasT[:], [128, 2, 16, 128], BF16, [b_biasT])
        dump("bmix", bmix[:], [128, 2048], F32, [b_bmix])

    def norm_transpose(xt, b_xt, gbc, b_gbc, hb, b_hb, hT_dst, b_hT_list, st_col):
        col = ss[:, st_col:st_col + 1]
        bsc = b_ssc[st_col]
        jk, bjk = (junk, b_junk) if st_col % 2 == 0 else (junkb, b_junkb)
        act(jk[:], xt, AF.Square, [b_xt], [bjk, bsc], accum=col)
        ts("dve", col, col, 1.0 / D, EPS, ALU.mult, ALU.add, [bsc], [bsc])
        tt("pool", col, col, nhalf[:, 0:1], ALU.pow, [bsc, b_nhalf], [bsc])
        stt("dve", hb, xt, col, gbc, ALU.mult, ALU.mult, [b_xt, bsc, b_gbc], [b_hb])
        for half in range(2):
            pt, bp = next_psb()
            for kk in range(8):
                k = half * 8 + kk
                P.op("pe", lambda e, o=pt[:, kk, :], i=hb[:, k * 128:(k + 1) * 128]:
                     e.transpose(o, i, ident[:]), [b_hb, b_ident], [bp], inc=(kk == 7))
            copy("act" if half == 0 else "dve", hT_dst[:, half * 8:(half + 1) * 8, :],
                 pt[:], [bp], b_hT_list[half * 8:(half + 1) * 8])

    b_kscr = [Buf("kscr%d" % g) for g in range(8)]
    b_vscr = [Buf("vscr%d" % g) for g in range(8)]

    with ExitStack() as s1:
        g1bc = sb("g1bc", [128, D], F32, s1)
        b_g1bc = Buf("g1bc")
        P.dma("sp", g1bc[:], g1_d.partition_broadcast(128), writes=[b_g1bc])
        wk = sb("wk", [128, 16, 512], BF16, s1)
        wv = sb("wv", [128, 16, 512], BF16, s1)
        wki = sb("wki", [128, 16, 128], BF16, s1)
        b_wk, b_wv, b_wki = Buf("wk"), Buf("wv"), Buf("wki")
        b_wkg = [Buf("wk%d" % g) for g in range(4)]
        for g in range(4):
            P.dma("pool", wk[:, :, g * 128:(g + 1) * 128],
                  w_in[:, C_K + g * 128:C_K + (g + 1) * 128].rearrange("(k p) c -> p k c", p=128),
                  writes=[b_wkg[g]])
        wtile_dma(wv[:], w_in[:, C_V:C_V + 512], [b_wv])
        P.dma("pool", wki[:, :, 0:64],
              w_in[:, C_KI:C_KI + 64].rearrange("(k p) c -> p k c", p=128),
              writes=[b_wki], shared=True)
        P.dma("pool", wki[:, :, 64:128],
              w_in[:, C_KI:C_KI + 64].rearrange("(k p) c -> p k c", p=128),
              writes=[b_wki], shared=True)
        xb = [sb("xb%d" % i, [128, D], F32, s1) for i in range(4)]
        b_xb = [Buf("xb%d" % i) for i in range(4)]
        hb = [sb("hb%d" % i, [128, D], BF16, s1) for i in range(4)]
        b_hb = [Buf("hb%d" % i) for i in range(4)]
        hT = [sb("hT%d" % i, [128, 16, 512], BF16, s1) for i in range(2)]
        b_hT = [[Buf("hT%d_%d" % (i, b)) for b in range(4)] for i in range(2)]
        kst = [sb("kst%d" % i, [128, 512], BF16, s1) for i in range(2)]
        b_kst = [Buf("kst%d" % i) for i in range(2)]
        nst = [0]
        ngrp = 8 if stage >= 1 else 0

        def p1_stats(grp):
            for b in range(4):
                blk = grp * 4 + b
                P.dma("sp", xb[b][:], x_sh[blk * 128:(blk + 1) * 128, :], writes=[b_xb[b]])
                col = ss[:, (blk % 8):(blk % 8) + 1]
                bsc = b_ssc[blk % 8]
                jk, bjk = (junk, b_junk) if b % 2 == 0 else (junkb, b_junkb)
                act(jk[:], xb[b][:], AF.Square, [b_xb[b]], [bjk, bsc], accum=col)
                ts("dve", col, col, 1.0 / D, EPS, ALU.mult, ALU.add, [bsc], [bsc])
                tt("pool", col, col, nhalf[:, 0:1], ALU.pow, [bsc, b_nhalf], [bsc])
                stt("dve", hb[b][:], xb[b][:], col, g1bc[:], ALU.mult, ALU.mult,
                    [b_xb[b], bsc, b_g1bc], [b_hb[b]])

        def p1_transposes(grp, blocks):
            hTt, bhT = hT[grp % 2], b_hT[grp % 2]
            for b in blocks:
                for half in range(2):
                    pt, bp = next_psb()
                    for kk in range(8):
                        k = half * 8 + kk
                        P.op("pe", lambda e, o=pt[:, kk, :], i=hb[b][:, k * 128:(k + 1) * 128]:
                             e.transpose(o, i, ident[:]), [b_hb[b], b_ident], [bp], inc=(kk == 7))
                    copy("act" if half == 0 else "dve",
                         hTt[:, half * 8:(half + 1) * 8, b * 128:(b + 1) * 128], pt[:], [bp], [bhT[b]])

        def p1_kT(grp):
            hTt, bhT = hT[grp % 2], b_hT[grp % 2]
            for g in range(4):
                pt, bp = next_psf()
                for k in range(16):
                    mm(pt[:], wk[:, k, g * 128:(g + 1) * 128], hTt[:, k, :], k == 0, k == 15,
                       [b_wkg[g]] + bhT, [bp])
                si = nst[0] % 2
                nst[0] += 1
                copy("act" if g % 2 == 0 else "dve", kst[si][:], pt[:], [bp], [b_kst[si]])
                P.dma("sp", kT_d[g, :, grp * 512:(grp + 1) * 512], kst[si][:],
                      reads=[b_kst[si]], writes=[b_kscr[grp]], shared=True)

        def p1_ki(grp):
            hTt, bhT = hT[grp % 2], b_hT[grp % 2]
            pt, bp = next_psf()
            for k in range(16):
                mm(pt[:], wki[:, k, :], hTt[:, k, :], k == 0, k == 15, [b_wki] + bhT, [bp])
            copy("act", kiT2[:, grp * 512:(grp + 1) * 512], pt[:], [bp], [b_kiT2[grp]])

        def p1_V(grp, blocks):
            hTt, bhT = hT[grp % 2], b_hT[grp % 2]
            for b in blocks:
                blk = grp * 4 + b
                pt, bp = next_psf()
                for k in range(16):
                    mm(pt[:], hTt[:, k, b * 128:(b + 1) * 128], wv[:, k, :], k == 0, k == 15,
                       [b_wv, bhT[b]], [bp])
                si = nst[0] % 2
                nst[0] += 1
                copy("dve" if b % 2 == 0 else "act", kst[si][:], pt[:], [bp], [b_kst[si]])
                P.dma("sp", V_d[blk * 128:(blk + 1) * 128, :], kst[si][:],
                      reads=[b_kst[si]], writes=[b_vscr[grp]], shared=True)

        if ngrp:
            p1_stats(0)
            p1_transposes(0, range(4))
        for grp in range(ngrp):
            nxt = grp + 1 < ngrp
            if nxt:
                p1_stats(grp + 1)
            p1_kT(grp)
            if nxt:
                p1_transposes(grp + 1, [0, 1])
            p1_ki(grp)
            p1_V(grp, [0, 1])
            if nxt:
                p1_transposes(grp + 1, [2, 3])
            p1_V(grp, [2, 3])
        dump("kiT2", kiT2[:], [128, SEQ], BF16, b_kiT2)

    P.barrier()
    nsb_run = nsb if stage >= 2 else 0
    for sbi in range(nsb_run):
        with ExitStack() as sS:
            U = sb("U%d" % sbi, [128, 8192], F32, sS)
            b_U = [Buf("U_%d" % i) for i in range(4)]
            qT = U[:, 0:4096].bitcast(BF16).rearrange("p (h t) -> p h t", h=16)
            gaz = U[:, 4096:8192].bitcast(BF16)
            gaT = gaz.rearrange("p (h t) -> p h t", h=16)
            x2 = U[:, :].rearrange("p (b c) -> p b c", b=4)
            bq = lambda h: b_U[h // 8]
            bgz = lambda q: b_U[2 + q // 2]
            M = sb("M%d" % sbi, [128, 16, 512], BF16, sS)
            b_M = [[Buf("M_%d_%d" % (g, lb)) for lb in range(4)] for g in range(16)]
            qiT2 = sb("qiT2_%d" % sbi, [128, 4, 512], BF16, sS)
            b_qi = [Buf("qi%d" % p) for p in range(4)]
            wi = sb("wi%d" % sbi, [128, 4, 8], F32, sS)
            b_wi = [Buf("wi%d" % lb) for lb in range(4)]
            own_blk = [4 * (4 * sbi + lb) + 3 for lb in range(4)]

            with ExitStack() as sA:
                g1bc = sb("g1bcA%d" % sbi, [128, D], F32, sA)
                b_g1bc = Buf("g1bcA")
                P.dma("sp", g1bc[:], g1_d.partition_broadcast(128), writes=[b_g1bc])
                hTa = sb("hTa%d" % sbi, [128, 16, 512], BF16, sA)
                b_hTa = [Buf("hTa%d" % lb) for lb in range(4)]
                xb = [sb("xbA%d_%d" % (sbi, i), [128, D], F32, sA) for i in range(2)]
                b_xb = [Buf("xbA%d" % i) for i in range(2)]
                hb = [sb("hbA%d_%d" % (sbi, i), [128, D], BF16, sA) for i in range(2)]
                b_hb = [Buf("hbA%d" % i) for i in range(2)]
                wt = [sb("wtA%d_%d" % (sbi, i), [128, 16, 512], BF16, sA) for i in range(2)]
                b_wt = [Buf("wtA%d" % i) for i in range(2)]
                wwi = sb("wwi%d" % sbi, [128, 16, 8], BF16, sA)
                b_wwi = Buf("wwi")
                t1 = [sb("t1_%d_%d" % (sbi, i), [128, 512], F32, sA) for i in range(2)]
                b_t1 = [Buf("t1_%d" % i) for i in range(2)]
                t2 = [sb("t2_%d_%d" % (sbi, i), [128, 512], F32, sA) for i in range(2)]
                b_t2 = [Buf("t2_%d" % i) for i in range(2)]
                stA = sb("stA%d" % sbi, [128, 2, 4, 4], F32, sA)
                b_stA = Buf("stA")
                st2 = sb("st2_%d" % sbi, [128, 4, 4], F32, sA)
                b_st2 = Buf("st2")


                tiles = ([("vs", c, C_VS + c * 512) for c in range(4)] +
                         [("q", c, C_Q + c * 512) for c in range(4)] +
                         [("u", c, C_U + c * 512) for c in range(4)] +
                         [("gb", c, C_GB + c * 512) for c in range(4)] +
                         [("ga", c, C_GA + c * 512) for c in range(4)] +
                         [("qi", 0, C_QI)])
                memset("dve", stA[:], 0.0, [b_stA])

                def issue(ti):
                    kind, c, col = tiles[ti]
                    wtile_dma(wt[ti % 2][:], w_in[:, col:col + 512], [b_wt[ti % 2]])

                gcount = [0]

                def gelu_core(pt, bp):
                    i = gcount[0] % 2
                    gcount[0] += 1
                    act(t1[i][:], pt[:], AF.Square, [bp], [b_t1[i]])
                    ts("dve", t1[i][:], t1[i][:], GA_, 1.0, ALU.mult, ALU.add, [b_t1[i]], [b_t1[i]])
                    tt("dve", t1[i][:], t1[i][:], pt[:], ALU.mult, [b_t1[i], bp], [b_t1[i]])
                    act(t2[i][:], t1[i][:], AF.Tanh, [b_t1[i]], [b_t2[i]], scale=GC)
                    return t2[i], b_t2[i]

                need_mix = []

                def do_mixing():
                    for lb in range(4):
                        for q in range(4):
                            pt, bp = next_psf()
                            for gg in range(4):
                                g = q * 4 + gg
                                mm(pt[:, gg * 128:(gg + 1) * 128],
                                   gaz[:, lb * 2048 + g * 128: lb * 2048 + (g + 1) * 128],
                                   wsT[:, g * 128:(g + 1) * 128], True, True,
                                   [bgz(lb), b_wsT], [bp], inc=(gg == 3))
                            for gg in range(4):
                                g = q * 4 + gg
                                stt("dve", M[:, g, lb * 128:(lb + 1) * 128],
                                    pt[:, gg * 128:(gg + 1) * 128], lngc[:, g:g + 1],
                                    bmix[:, g * 128:(g + 1) * 128], ALU.mult, ALU.add,
                                    [bp, b_cst, b_bmix], [b_M[g][lb]])
                    if sbi == 0:
                        dump("mixT", M[:], [128, 16, 512], BF16, [b for r_ in b_M for b in r_])

                issue(0)
                issue(1)
                for lb in range(4):
                    blk = own_blk[lb]
                    xi = lb % 2
                    P.dma("sp", xb[xi][:], x_sh[blk * 128:(blk + 1) * 128, :], writes=[b_xb[xi]])
                    norm_transpose(xb[xi][:], b_xb[xi], g1bc[:], b_g1bc, hb[xi][:], b_hb[xi],
                                   hTa[:, :, lb * 128:(lb + 1) * 128], [b_hTa[lb]] * 16, lb)
                P.dma("pool", wwi[:], w_in[:, C_WI:C_WI + 8].rearrange("(k p) c -> p k c", p=128),
                      writes=[b_wwi])
                for ti, (kind, c, col) in enumerate(tiles):
                    if ti >= 1 and ti + 1 < len(tiles):
                        issue(ti + 1)
                    w, bw = wt[ti % 2], b_wt[ti % 2]
                    if kind == "vs":
                        for lb in range(4):
                            pt, bp = next_psf()
                            for k in range(16):
                                mm(pt[:], hTa[:, k, lb * 128:(lb + 1) * 128], w[:, k, :],
                                   k == 0, k == 15, [b_hTa[lb], bw], [bp])
                            tz, btz = gelu_core(pt, bp)
                            zdst = gaz[:, lb * 2048 + c * 512: lb * 2048 + (c + 1) * 512]
                            stt("dve", zdst, tz[:], 1.0, pt[:], ALU.add, ALU.mult,
                                [btz, bp], [bgz(lb)])
                            act(junk[:, 0:512], zdst, AF.Identity, [bgz(lb), b_stA],
                                [b_junk, b_stA], accum=stA[:, 0, lb, c:c + 1])
                            act(junk[:, 512:1024], zdst, AF.Square, [bgz(lb), b_stA],
                                [b_junk, b_stA], accum=stA[:, 1, lb, c:c + 1])
                        if c == 3:
                            for lb in range(4):
                                mu, var, tmp = (st2[:, lb, 0:1], st2[:, lb, 1:2], st2[:, lb, 2:3])
                                reduce(mu, stA[:, 0, lb, :], ALU.add, [b_stA], [b_st2])
                                reduce(var, stA[:, 1, lb, :], ALU.add, [b_stA], [b_st2])
                                ts("dve", mu, mu, 1.0 / D, None, ALU.mult, None, [b_st2], [b_st2])
                                tt("dve", tmp, mu, mu, ALU.mult, [b_st2], [b_st2])
                                stt("dve", var, var, 1.0 / D, tmp, ALU.mult, ALU.subtract,
                                    [b_st2], [b_st2])
                                ts("dve", var, var, 4.0 * EPS, None, ALU.add, None, [b_st2], [b_st2])
                                tt("pool", var, var, nhalf[:, 0:1], ALU.pow, [b_st2, b_nhalf], [b_st2])
                                zb = gaz[:, lb * 2048:(lb + 1) * 2048]
                                ts("dve", zb, zb, mu, var, ALU.subtract, ALU.mult,
                                   [b_st2, bgz(lb)], [bgz(lb)])
                            need_mix.append(1)
                    else:
                        ncc = 4
                        if kind == "u" and need_mix:
                            need_mix.pop()
                            do_mixing()
                        for cc in range(ncc):
                            g = c * 4 + cc
                            pt, bp = next_psf()
                            for k in range(16):
                                mm(pt[:], w[:, k, cc * 128:(cc + 1) * 128], hTa[:, k, :],
                                   k == 0, k == 15, b_hTa + [bw], [bp])
                            if kind == "u":
                                tz, btz = gelu_core(pt, bp)
                                stt("dve", tz[:], tz[:], 1.0, pt[:], ALU.add, ALU.mult,
                                    [btz, bp], [btz])
                                stt("dve", M[:, g, :], tz[:], 0.25, M[:, g, :], ALU.mult, ALU.mult,
                                    [btz] + b_M[g], b_M[g])
                            elif kind == "gb":
                                i = gcount[0] % 2
                                gcount[0] += 1
                                act(t2[i][:], pt[:], AF.Tanh, [bp], [b_t2[i]], scale=0.5)
                                stt("dve", M[:, g, :], t2[i][:], 1.0, M[:, g, :], ALU.add, ALU.mult,
                                    [b_t2[i]] + b_M[g], b_M[g])
                            elif kind == "ga":
                                act(gaT[:, g, :], pt[:], AF.Tanh, [bp], [bgz(g // 4)], scale=0.5)
                            elif kind == "q":
                                copy("act" if cc % 2 == 0 else "dve", qT[:, g, :], pt[:], [bp], [bq(g)])
                            elif kind == "qi":
                                copy("act" if cc % 2 == 0 else "dve", qiT2[:, cc, :], pt[:], [bp],
                                     [b_qi[cc]])
                for lb in range(4):
                    pt, bp = next_psf()
                    for k in range(16):
                        mm(pt[:, 0:8], hTa[:, k, lb * 128:(lb + 1) * 128], wwi[:, k, :],
                           k == 0, k == 15, [b_hTa[lb], b_wwi], [bp])
                    copy("dve", wi[:, lb, :], pt[:, 0:8], [bp], [b_wi[lb]])
                if sbi == 0:
                    dump("sgT", M[:], [128, 16, 512], BF16, [b for r_ in b_M for b in r_])
                    dump("U", U[:], [128, 8192], F32, b_U)
                    dump("qiT2", qiT2[:], [128, 4, 512], BF16, b_qi)
                    dump("wi", wi[:], [128, 4, 8], F32, b_wi)

            P.barrier()
            if stage < 3:
                P.barrier()
                continue
            with ExitStack() as sB:
                S = sb("S%d" % sbi, [128, SEQ], F32, sB)
                b_S = Buf("S")
                mask = sb("mask%d" % sbi, [128, SEQ], BF16, sB)
                b_mask = Buf("mask")
                maskT = sb("maskT%d" % sbi, [128, 32, 128], BF16, sB)
                b_maskT = Buf("maskT")
                R = [sb("R%d_%d" % (sbi, h), [128, 512], BF16, sB) for h in range(8)]
                b_R = [Buf("R%d" % h) for h in range(8)]
                Dw = sb("Dw%d" % sbi, [128, 8, 128], BF16, sB)
                b_Dw = Buf("Dw")
                kTg = [sb("kTg%d_%d" % (sbi, i), [128, SEQ], BF16, sB) for i in range(2)]
                b_kTg = [Buf("kTg%d" % i) for i in range(2)]
                Vt = [sb("Vt%d_%d" % (sbi, i), [128, 32, 129], BF16, sB) for i in range(2)]
                b_Vt = [Buf("Vt%d" % i) for i in range(2)]
                for i in range(2):
                    memset("dve", Vt[i][:, :, 128:129], 1.0, [b_Vt[i]])
                Et = [sb("Et%d_%d" % (sbi, i), [128, 4, 128], BF16, sB) for i in range(3)]
                b_Et = [Buf("Et%d" % i) for i in range(3)]
                PTt = [sb("PT%d_%d" % (sbi, i), [128, 4, 128], BF16, sB) for i in range(3)]
                b_PT = [Buf("PT%d" % i) for i in range(3)]
                atok = sb("atok%d" % sbi, [128, D], BF16, sB)
                b_atok = Buf("atok")
                bs = sb("bs%d" % sbi, [128, 8], F32, sB)
                b_bs = Buf("bs")
                steps = sb("steps%d" % sbi, [128, NBIS + 1], F32, sB)
                b_steps = Buf("steps")
                rden = sb("rden%d" % sbi, [128, 4], F32, sB)
                b_rden = Buf("rden")
                mtmp = sb("mtmp%d" % sbi, [128, 8, 128], F32, sB)
                b_mtmp = Buf("mtmp")

                maskT2 = [maskT, sb("maskTb%d" % sbi, [128, 32, 128], BF16, sB)]
                b_maskT2 = [b_maskT, Buf("maskTb")]
                S2 = [S, sb("Sb%d" % sbi, [128, SEQ], F32, sB)]
                b_S2 = [b_S, Buf("Sb")]
                den = sb("den%d" % sbi, [128, 16], F32, sB)
                b_den = Buf("den")
                PS_ACC = (psb[1][0][:].rearrange("p a b -> p (a b)").bitcast(F32), psb[1][1])
                PS_TR = psb[0]

                def F1(lb):
                    m = 4 * sbi + lb
                    nkt = m + 1
                    tok = slice(lb * 128, (lb + 1) * 128)
                    Sx, bSx = S2[lb % 2], b_S2[lb % 2]
                    for h in range(8):
                        ts("dve", Dw[:, h, :], ident[:], wi[:, lb, h:h + 1], None, ALU.mult, None,
                           [b_ident, b_wi[lb]], [b_Dw])
                    yield
                    for kt in range(nkt):
                        ks = slice(kt * 512, (kt + 1) * 512)
                        for hp in range(4):
                            for h in (2 * hp, 2 * hp + 1):
                                rows = slice(64 * (h % 2), 64 * (h % 2) + 64)
                                pt, bp = psf[4 + h % 2]
                                mm(pt[:], qiT2[rows, h // 2, tok], kiT2[rows, ks], True, True,
                                   [b_qi[h // 2], b_kiT2[kt]], [bp])
                            for h in (2 * hp, 2 * hp + 1):
                                pt, bp = psf[4 + h % 2]
                                act(R[h][:], pt[:], AF.Relu, [bp], [b_R[h]])
                            yield
                            yield
                        pa, bpa = PS_ACC
                        for h in range(8):
                            mm(pa, Dw[:, h, :], R[h][:], h == 0, h == 7, [b_Dw, b_R[h]], [bpa])
                            if h == 7:
                                copy("act", Sx[:, ks], pa, [bpa], [bSx])
                            yield

                def F2(lb):
                    m = 4 * sbi + lb
                    nkb = 4 * m + 4
                    L = nkb * 128
                    Sx, bSx = S2[lb % 2], b_S2[lb % 2]
                    mT, b_mT = maskT2[lb % 2], b_maskT2[lb % 2]
                    rmax, lo, wid, mid, cnt, sgn = (bs[:, i:i + 1] for i in range(6))
                    reduce(rmax, Sx[:, 0:L], ALU.max, [bSx], [b_bs])
                    reduce(lo, Sx[:, 0:L], ALU.min, [bSx], [b_bs])
                    yield
                    tt("dve", wid, rmax, lo, ALU.subtract, [b_bs], [b_bs])
                    stt("dve", lo, wid, -0.02, lo, ALU.mult, ALU.add, [b_bs], [b_bs])
                    ts("dve", wid, wid, 1.04, None, ALU.mult, None, [b_bs], [b_bs])
                    ts("dve", steps[:], pow2[:], wid, None, ALU.mult, None, [b_cst, b_bs], [b_steps])
                    tt("dve", mid, lo, steps[:, 0:1], ALU.add, [b_bs, b_steps], [b_bs])
                    tt("dve", Sx[:, 0:512], Sx[:, 0:512], fake[:], ALU.add, [bSx, b_cst], [bSx])
                    tt("dve", Sx[:, L - 128:L], Sx[:, L - 128:L], negm[:], ALU.add, [bSx, b_cst], [bSx])
                    yield
                    for j in range(NBIS):
                        ts("dve", mask[:, 0:L], Sx[:, 0:L], mid, 0.0, ALU.is_ge, ALU.add,
                           [bSx, b_bs], [b_mask, b_bs], accum=cnt)
                        ts("dve", sgn, cnt, TOPK - 0.5, None, ALU.is_ge, None, [b_bs], [b_bs])
                        stt("dve", lo, sgn, steps[:, j:j + 1], lo, ALU.mult, ALU.add,
                            [b_bs, b_steps], [b_bs])
                        tt("dve", mid, lo, steps[:, j + 1:j + 2], ALU.add, [b_bs, b_steps], [b_bs])
                        yield
                    ts("dve", mask[:, 0:L], Sx[:, 0:L], lo, -BIGM, ALU.is_lt, ALU.mult,
                       [bSx, b_bs], [b_mask])
                    yield
                    for jb in range(0, nkb, 8):
                        nj = min(8, nkb - jb)
                        pt, bp = PS_TR
                        for jj in range(nj):
                            j = jb + jj
                            P.op("pe", lambda e, o=pt[:, jj, :], i=mask[:, j * 128:(j + 1) * 128]:
                                 e.transpose(o, i, ident[:]), [b_mask, b_ident], [bp],
                                 inc=(jj == nj - 1))
                        copy("dve", mT[:, jb:jb + nj, :], pt[:, 0:nj, :], [bp], [b_mT])
                        yield
                    if sbi == 0 and lb == 1:
                        dump("S", Sx[:, 0:L], [128, L], F32, [bSx])
                        dump("lo", bs[:, 0:6], [128, 6], F32, [b_bs])

                def back(lb):
                    m = 4 * sbi + lb
                    nkb = 4 * m + 4
                    nkt = m + 1
                    L = nkb * 128
                    tok = slice(lb * 128, (lb + 1) * 128)
                    mT, b_mT = maskT2[lb % 2], b_maskT2[lb % 2]
                    for g in range(4):
                        slot = (lb * 4 + g) % 2
                        P.dma("sp", kTg[slot][:, 0:L], kT_d[g, :, 0:L],
                              reads=b_kscr[0:nkt], writes=[b_kTg[slot]])
                        P.dma("sp", Vt[slot][:, 0:nkb, 0:128],
                              V_d[0:L, g * 128:(g + 1) * 128].rearrange("(j s) d -> s j d", s=128),
                              reads=b_vscr[0:nkt], writes=[b_Vt[slot]])
                        po = [psf[2], psf[3]]
                        lgs = {}

                        def qk(j):
                            lg, blg = psf[rot["a"] % 2]
                            rot["a"] += 1
                            near = j >= nkb - 2
                            lg3 = lg[:].rearrange("p (r t) -> p r t", r=4)
                            mm(lg3, kTg[slot][:, j * 128:(j + 1) * 128],
                               qT[:, 4 * g:4 * g + 4, tok], True, False,
                               [b_kTg[slot], bq(4 * g)], [blg])
                            mm(lg3, ident[:],
                               mT[:, j, :].unsqueeze(1).to_broadcast([128, 4, 128]), False, not near,
                               [b_ident, b_mT], [blg])
                            if near:
                                kk = nkb - 1 - j
                                mm(lg3, ident[:], biasT[:, kk, 4 * g:4 * g + 4, :], False, True,
                                   [b_ident, b_biasT], [blg])
                            lgs[j] = (lg, blg)

                        qk(0)
                        for j in range(nkb):
                            if j + 1 < nkb:
                                qk(j + 1)
                            lg, blg = lgs.pop(j)
                            e = j % 3
                            act(PTt[e][:], lg[:].rearrange("p (r t) -> p r t", r=4), AF.Exp,
                                [blg], [b_PT[e]], scale=SCALE)
                            for r in range(4):
                                pb, bpb = po[r // 2]
                                c0 = (r % 2) * 129
                                mm(pb[:, c0:c0 + 129], PTt[e][:, r, :], Vt[slot][:, j, :],
                                   (j == 0 and r % 2 == 0), (j == nkb - 1 and r % 2 == 1),
                                   [b_PT[e], b_Vt[slot]], [bpb],
                                   inc=(r == 3 or j == nkb - 1))
                            yield
                        for half in range(2):
                            pb, bpb = po[half]
                            v3 = pb[:, 0:258].rearrange("p (r c) -> p r c", c=129)
                            h0 = 4 * g + 2 * half
                            copy("act", atok[:, h0 * 128:(h0 + 2) * 128].rearrange("p (r c) -> p r c", r=2),
                                 v3[:, :, 0:128], [bpb], [b_atok])
                            copy("act", den[:, h0:h0 + 2], v3[:, :, 128], [bpb], [b_den])
                        yield
                    P.op("dve", lambda e: e.reciprocal(den[:], den[:]), [b_den], [b_den])
                    a3 = atok[:].rearrange("p (h c) -> p h c", h=16)
                    tt("dve", a3, a3, den[:, :].unsqueeze(2).to_broadcast([128, 16, 128]), ALU.mult,
                       [b_atok, b_den], [b_atok])
                    if sbi == 0 and lb == 1:
                        dump("atok", atok[:], [128, D], BF16, [b_atok])
                    for hh in range(2):
                        pt, bp = PS_TR
                        for jj in range(8):
                            h = hh * 8 + jj
                            P.op("pe", lambda e, o=pt[:, jj, :], i=atok[:, h * 128:(h + 1) * 128]:
                                 e.transpose(o, i, ident[:]), [b_atok, b_ident], [bp], inc=(jj == 7))
                        hs = slice(hh * 8, hh * 8 + 8)
                        stt("dve", mtmp[:], gaT[:, hs, tok], 1.0, pt[:], ALU.add, ALU.mult,
                            [bgz(2 * hh), bgz(2 * hh + 1), bp], [b_mtmp])
                        mb = [b_M[h][lb] for h in range(hh * 8, hh * 8 + 8)]
                        stt("dve", M[:, hs, tok], mtmp[:], 0.5, M[:, hs, tok], ALU.mult, ALU.add,
                            [b_mtmp] + mb, mb)
                        yield

                def len_F1(lb):
                    return 1 + 16 * (4 * sbi + lb + 1)

                def len_F2(lb):
                    return NBIS + 4 + (4 * (4 * sbi + lb) + 4 + 7) // 8

                def len_back(lb):
                    return 4 * (4 * (4 * sbi + lb) + 4) + 4 + 2

                def run_par(gens):
                    master, nm = gens[0]
                    slaves = [[g, n, 0] for g, n in gens[1:]]
                    done = 0
                    for _ in master:
                        done += 1
                        for sl in slaves:
                            while sl[0] is not None and sl[2] * nm < done * sl[1]:
                                try:
                                    next(sl[0])
                                    sl[2] += 1
                                except StopIteration:
                                    sl[0] = None
                    for sl in slaves:
                        if sl[0] is not None:
                            for _ in sl[0]:
                                pass

                run_par([(F1(0), len_F1(0))])
                run_par([(F2(0), len_F2(0)), (F1(1), len_F1(1))])
                for lb in range(4):
                    gl = [(back(lb), len_back(lb))]
                    if lb + 1 < 4:
                        gl.append((F2(lb + 1), len_F2(lb + 1)))
                    if lb + 2 < 4:
                        gl.append((F1(lb + 2), len_F1(lb + 2)))
                    run_par(gl)
                if sbi == 0:
                    dump("merged", M[:], [128, 16, 512], BF16, [b for r_ in b_M for b in r_])

            P.barrier()
            if stage < 4:
                P.barrier()
                continue
            with ExitStack() as sC:
                wt = [sb("wtC%d_%d" % (sbi, i), [128, 16, 512], BF16, sC) for i in range(4)]
                b_wt = [Buf("wtC%d" % i) for i in range(4)]
                g2bc = sb("g2bc%d" % sbi, [128, D], F32, sC)
                b_g2bc = Buf("g2bc")
                hbC = [sb("hbC%d_%d" % (sbi, i), [128, D], BF16, sC) for i in range(2)]
                b_hbC = [Buf("hbC%d" % i) for i in range(2)]
                for c in range(4):
                    wtile_dma(wt[c][:], w_out[:, c * 512:(c + 1) * 512], [b_wt[c]])
                P.dma("sp", g2bc[:], g2_d.partition_broadcast(128), writes=[b_g2bc])
                for lb in range(4):
                    blk = own_blk[lb]
                    P.dma("sp", x2[:, lb, :], x_sh[blk * 128:(blk + 1) * 128, :], writes=[b_U[lb]])

                def c_stats(lb):
                    col = ss[:, lb:lb + 1]
                    bsc = b_ssc[lb]
                    jk, bjk = (junk, b_junk) if lb % 2 == 0 else (junkb, b_junkb)
                    act(jk[:], x2[:, lb, :], AF.Square, [b_U[lb]], [bjk, bsc], accum=col)
                    ts("dve", col, col, 1.0 / D, EPS, ALU.mult, ALU.add, [bsc], [bsc])
                    tt("pool", col, col, nhalf[:, 0:1], ALU.pow, [bsc, b_nhalf], [bsc])
                    stt("dve", hbC[lb % 2][:], x2[:, lb, :], col, g2bc[:], ALU.mult, ALU.mult,
                        [b_U[lb], bsc, b_g2bc], [b_hbC[lb % 2]])

                def c_transposes(lb):
                    for half in range(2):
                        pt, bp = next_psb()
                        for kk in range(8):
                            k = half * 8 + kk
                            P.op("pe", lambda e, o=pt[:, kk, :], i=hbC[lb % 2][:, k * 128:(k + 1) * 128]:
                                 e.transpose(o, i, ident[:]), [b_hbC[lb % 2], b_ident], [bp], inc=(kk == 7))
                        copy("act" if half == 0 else "dve",
                             M[:, half * 8:(half + 1) * 8, lb * 128:(lb + 1) * 128], pt[:], [bp],
                             [b_M[k][lb] for k in range(half * 8, half * 8 + 8)])

                for lb in range(4):
                    for c in range(4):
                        pt, bp = next_psf()
                        for k in range(16):
                            mm(pt[:], M[:, k, lb * 128:(lb + 1) * 128], wt[c][:, k, :], k == 0, k == 15,
                               [b_M[k][lb], b_wt[c]], [bp])
                        xs = x2[:, lb, c * 512:(c + 1) * 512]
                        tt("dve", xs, xs, pt[:], ALU.add, [b_U[lb], bp], [b_U[lb]])
                    if sbi == 0 and lb == 3:
                        dump("x2", U[:], [128, 8192], F32, b_U)
                    c_stats(lb)
                    if lb >= 1:
                        c_transposes(lb - 1)
                c_transposes(3)

            if stage < 5:
                P.barrier()
                continue
            with ExitStack() as sD:
                ffT = sb("ffT%d" % sbi, [128, 16, 512], BF16, sD)
                b_ffT = [Buf("ffT%d" % i) for i in range(16)]
                wt = [sb("wtD%d_%d" % (sbi, i), [128, 16, 512], BF16, sD) for i in range(3)]
                b_wt = [Buf("wtD%d" % i) for i in range(3)]
                rt = [sb("rt%d_%d" % (sbi, i), [128, 512], F32, sD) for i in range(2)]
                b_rt = [Buf("rt%d" % i) for i in range(2)]
                h2T = M
                b_h2all = [b for r_ in b_M for b in r_]
                srcs = []
                for seg in range(4):
                    for c in range(4):
                        srcs.append(("w1", seg, c, w_ff1[:, seg * 2048 + c * 512: seg * 2048 + (c + 1) * 512]))
                    for c in range(4):
                        srcs.append(("w2", seg, c, w_ff2[seg * 2048:(seg + 1) * 2048, c * 512:(c + 1) * 512]))
                wtile_dma(wt[0][:], srcs[0][3], [b_wt[0]])
                wtile_dma(wt[1][:], srcs[1][3], [b_wt[1]])
                nrt = 0
                for ti, (kind, seg, c, src) in enumerate(srcs):
                    if ti + 2 < len(srcs):
                        wtile_dma(wt[(ti + 2) % 3][:], srcs[ti + 2][3], [b_wt[(ti + 2) % 3]])
                    w, bw = wt[ti % 3], b_wt[ti % 3]
                    if kind == "w1":
                        for cc in range(4):
                            fc = c * 4 + cc
                            pt, bp = next_psf()
                            for k in range(16):
                                mm(pt[:], w[:, k, cc * 128:(cc + 1) * 128], h2T[:, k, :], k == 0, k == 15,
                                   [b_M[k][lb] for lb in range(4)] + [bw], [bp])
                            i = nrt % 2
                            nrt += 1
                            act(rt[i][:], pt[:], AF.Relu, [bp], [b_rt[i]])
                            tt("dve", ffT[:, fc, :], rt[i][:], rt[i][:], ALU.mult, [b_rt[i]], [b_ffT[fc]])
                    else:
                        for lb in range(4):
                            pt, bp = next_psf()
                            for fc in range(16):
                                mm(pt[:], ffT[:, fc, lb * 128:(lb + 1) * 128], w[:, fc, :],
                                   fc == 0, fc == 15, [b_ffT[fc], bw], [bp])
                            xs = x2[:, lb, c * 512:(c + 1) * 512]
                            tt("dve", xs, xs, pt[:], ALU.add, [b_U[lb], bp], [b_U[lb]])
                if sbi == 0:
                    dump("x3", U[:], [128, 8192], F32, b_U)

            P.barrier()
            with ExitStack() as sE:
                fgbc = sb("fgbc%d" % sbi, [128, D], F32, sE)
                b_fgbc = Buf("fgbc")
                P.dma("sp", fgbc[:], fg_d.partition_broadcast(128), writes=[b_fgbc])
                ot = [sb("ot%d_%d" % (sbi, i), [128, D], F32, sE) for i in range(2)]
                b_ot = [Buf("ot%d" % i) for i in range(2)]
                for lb in range(4):
                    col = ss[:, lb:lb + 1]
                    bsc = b_ssc[lb]
                    jk, bjk = (junk, b_junk) if lb % 2 == 0 else (junkb, b_junkb)
                    act(jk[:], x2[:, lb, :], AF.Square, [b_U[lb]], [bjk, bsc], accum=col)
                    ts("dve", col, col, 1.0 / D, EPS, ALU.mult, ALU.add, [bsc], [bsc])
                    tt("pool", col, col, nhalf[:, 0:1], ALU.pow, [bsc, b_nhalf], [bsc])
                    o, bo = ot[lb % 2], b_ot[lb % 2]
                    stt("dve", o[:], x2[:, lb, :], col, fgbc[:], ALU.mult, ALU.mult,
                        [b_U[lb], bsc, b_fgbc], [bo])
                    row = (4 * sbi + lb) * 128
                    P.dma("sp", y_out[row:row + 128, :], o[:], reads=[bo])

            P.barrier()
        P.barrier()
    P.finish()
    P.emit()
    es.close()
    return nc


def _rel_bucket_np(dist):
    n = np.maximum(dist, 0)
    max_exact = 16
    nf = np.maximum(n, 1).astype(np.float32)
    large = max_exact + (np.log(nf / max_exact) / np.log(128 / max_exact) * (32 - max_exact)).astype(np.int32)
    large = np.minimum(large, 31)
    return np.where(n < max_exact, n, large)


def make_in_maps(inputs):
    x = np.asarray(inputs["x"], np.float32)
    rel_bias = np.asarray(inputs["rel_bias"], np.float32)
    f = lambda k: np.ascontiguousarray(np.asarray(inputs[k], np.float32))
    w_in = f("w_in")[0]
    w_out = f("w_out")[0]
    w_ff1 = f("w_ff1")[0]
    w_ff2 = f("w_ff2")[0]
    s_idx = np.arange(128)[:, None]
    t_idx = np.arange(128)[None, :]
    bidx = np.stack([_rel_bucket_np(t_idx - s_idx + 128 * k) for k in range(2)], axis=1)
    braw = np.ascontiguousarray(rel_bias[bidx].transpose(0, 1, 3, 2)).astype(np.float32)
    cfar = np.ascontiguousarray(np.broadcast_to(rel_bias[31][None, :], (128, 16))).astype(np.float32)
    common = {
        "w_in": w_in, "w_out": w_out, "w_ff1": w_ff1, "w_ff2": w_ff2,
        "norm1_g": f("norm1_g").reshape(1, D), "norm2_g": f("norm2_g").reshape(1, D),
        "final_g": f("final_g").reshape(1, D),
        "lng_col": np.ascontiguousarray(f("sgu_ln_g").reshape(16, 128).T),
        "lnb_col": np.ascontiguousarray(f("sgu_ln_b").reshape(16, 128).T),
        "sgub_row": f("sgu_b").reshape(1, D),
        "wsT": np.ascontiguousarray(f("sgu_w")[0].transpose(2, 0, 1)).reshape(128, 2048),
        "trilT": (s_idx <= t_idx).astype(np.float32),
        "ident_h": np.eye(128, dtype=np.float32).astype(ml_dtypes.bfloat16),
        "negmask": np.where(np.arange(128)[None, :] <= np.arange(128)[:, None], 0.0, NEG).astype(np.float32),
        "braw": braw.reshape(128, 4096), "cfar": cfar,
        "pow2": np.ascontiguousarray(np.broadcast_to(
            (0.5 ** np.arange(1, NBIS + 2, dtype=np.float64))[None, :], (128, NBIS + 1))).astype(np.float32),
    }
    in_maps = []
    for c in range(8):
        b, r = c // 4, c % 4
        pad = 3 - r
        xs = np.zeros((SEQ, D), np.float32)
        n = SEQ - pad * 128
        xs[pad * 128:] = x[b, :n]
        fake = np.zeros((128, 512), np.float32)
        fake[:, :pad * 128] = NEG
        m = dict(common)
        m["x_sh"] = xs
        m["fakemask"] = fake
        in_maps.append(m)
    return in_maps


def kernel(**inputs):
    in_maps = make_in_maps(inputs)
    nc = build_program()
    res = run_bass_kernel_spmd(nc, in_maps, core_ids=list(range(8)))
    out = np.zeros((2, SEQ, D), np.float32)
    for c in range(8):
        b, r = c // 4, c % 4
        y = np.asarray(res.results[c]["y"], np.float32)
        for m in range(8):
            blk = 4 * m + r
            out[b, blk * 128:(blk + 1) * 128] = y[m * 128:(m + 1) * 128]
    return out
```
